# Optimizing a Trainium2 kernel written in Bass

```python
import math
import jax, jax.numpy as jnp
from jax import lax
import numpy as np

D_MODEL = 2048
BATCH = 8
SEQ = 4096
DEPTH = 4

N_MIXERS = 2
N_S5_LAYERS = (DEPTH + N_MIXERS - 1) // N_MIXERS
N_ATTN_LAYERS = DEPTH // N_MIXERS

S5_GROUP = 16
S5_GROUPS = D_MODEL // S5_GROUP
S5_STATE = 64
S5_DIRS = 2
S5_DT_MIN = 1e-3
S5_DT_MAX = 1e-1

HEAD_DIM = 64
N_Q_HEADS = D_MODEL // HEAD_DIM
N_KV_HEADS = 4
GQA_GROUP = N_Q_HEADS // N_KV_HEADS
WINDOW = 128
BLOCK = 128
ROPE_THETA = 10000.0

D_FF = 4 * D_MODEL

NORM_EPS = 1e-6
NEG_INF = -1e30

kernel_name = "hybrid_s5_swa_sqrelu_encoder"


def rms_norm(x, gain):
    xf = x.astype(jnp.float32)
    y = xf * lax.rsqrt(jnp.mean(xf * xf, axis=-1, keepdims=True) + NORM_EPS)
    return (y * gain.astype(jnp.float32)).astype(x.dtype)


def _ssm_combine(e1, e2):
    a1r, a1i, b1r, b1i = e1
    a2r, a2i, b2r, b2i = e2
    ar = a1r * a2r - a1i * a2i
    ai = a1r * a2i + a1i * a2r
    br = a2r * b1r - a2i * b1i + b2r
    bi = a2r * b1i + a2i * b1r + b2i
    return ar, ai, br, bi


def s5_mixer(u, a_re, a_im, log_step, b_re, b_im, c_re, c_im, d_skip, w_glu_a, w_glu_b):
    bsz, seq, dm = u.shape
    f32 = jnp.float32
    ar = a_re.astype(f32)
    ai = a_im.astype(f32)
    step = jnp.exp(log_step.astype(f32))[..., None]
    mag = jnp.exp(ar * step)
    lb_re = mag * jnp.cos(ai * step)
    lb_im = mag * jnp.sin(ai * step)
    den = ar * ar + ai * ai
    nr = lb_re - 1.0
    ni = lb_im
    f_re = (nr * ar + ni * ai) / den
    f_im = (ni * ar - nr * ai) / den
    br_ = b_re.astype(f32)
    bi_ = b_im.astype(f32)
    bb_re = f_re[..., None] * br_ - f_im[..., None] * bi_
    bb_im = f_re[..., None] * bi_ + f_im[..., None] * br_
    cr = c_re.astype(f32)
    ci = c_im.astype(f32)

    uf = u.astype(f32)
    ug = uf.reshape(bsz, seq, S5_GROUPS, S5_GROUP)
    state_shape = (seq, S5_GROUPS, S5_STATE)

    def per_example(ue):
        bu_re = jnp.einsum('lgh,ngph->nlgp', ue, bb_re)
        bu_im = jnp.einsum('lgh,ngph->nlgp', ue, bb_im)
        y = jnp.zeros((seq, S5_GROUPS, S5_GROUP), f32)
        for d, rev in ((0, False), (1, True)):
            elems = (jnp.broadcast_to(lb_re[d], state_shape),
                     jnp.broadcast_to(lb_im[d], state_shape),
                     bu_re[d], bu_im[d])
            _, _, s_re, s_im = lax.associative_scan(_ssm_combine, elems, reverse=rev, axis=0)
            y = y + jnp.einsum('lgp,ghp->lgh', s_re, cr[d]) - jnp.einsum('lgp,ghp->lgh', s_im, ci[d])
        return y

    y = lax.map(per_example, ug).reshape(bsz, seq, dm)
    y = y + d_skip.astype(f32) * uf
    z = jax.nn.gelu(y).astype(u.dtype)
    return (z @ w_glu_a) * jax.nn.sigmoid(z @ w_glu_b)


def head_rms_norm(x, gain):
    xf = x.astype(jnp.float32)
    y = xf * lax.rsqrt(jnp.mean(xf * xf, axis=-1, keepdims=True) + NORM_EPS)
    return y * gain.astype(jnp.float32)


def apply_rotary(x, positions):
    half = HEAD_DIM // 2
    inv_freq = ROPE_THETA ** (-jnp.arange(0, half, dtype=jnp.float32) * 2.0 / HEAD_DIM)
    ang = positions.astype(jnp.float32)[..., None] * inv_freq
    cos = jnp.cos(ang)[:, :, None, :]
    sin = jnp.sin(ang)[:, :, None, :]
    x1 = x[..., :half]
    x2 = x[..., half:]
    return jnp.concatenate([x1 * cos - x2 * sin, x2 * cos + x1 * sin], axis=-1)


def windowed_gqa(h, positions, w_qkv, q_gain, k_gain, sink, w_o):
    bsz, seq, _ = h.shape
    nb = seq // BLOCK
    qkv = h @ w_qkv
    q_w = N_Q_HEADS * HEAD_DIM
    kv_w = N_KV_HEADS * HEAD_DIM
    q = qkv[..., :q_w].reshape(bsz, seq, N_Q_HEADS, HEAD_DIM)
    k = qkv[..., q_w:q_w + kv_w].reshape(bsz, seq, N_KV_HEADS, HEAD_DIM)
    v = qkv[..., q_w + kv_w:].reshape(bsz, seq, N_KV_HEADS, HEAD_DIM)
    q = apply_rotary(head_rms_norm(q, q_gain), positions).astype(h.dtype)
    k = apply_rotary(head_rms_norm(k, k_gain), positions).astype(h.dtype)

    q = q.reshape(bsz, nb, BLOCK, N_KV_HEADS, GQA_GROUP, HEAD_DIM)
    pad = ((0, 0), (BLOCK, BLOCK), (0, 0), (0, 0))
    kb = jnp.pad(k, pad).reshape(bsz, nb + 2, BLOCK, N_KV_HEADS, HEAD_DIM)
    vb = jnp.pad(v, pad).reshape(bsz, nb + 2, BLOCK, N_KV_HEADS, HEAD_DIM)
    k_win = jnp.concatenate([kb[:, :-2], kb[:, 1:-1], kb[:, 2:]], axis=2)
    v_win = jnp.concatenate([vb[:, :-2], vb[:, 1:-1], vb[:, 2:]], axis=2)

    scale = HEAD_DIM ** -0.5
    scores = jnp.einsum('bnqhgd,bnkhd->bnhgqk', q, k_win).astype(jnp.float32) * scale

    q_idx = jnp.arange(BLOCK)[:, None]
    k_off = jnp.arange(3 * BLOCK)[None, :] - BLOCK
    band = jnp.abs(k_off - q_idx) <= WINDOW
    key_abs = jnp.arange(nb)[:, None] * BLOCK + k_off
    in_range = (key_abs >= 0) & (key_abs < seq)
    mask = band[None, :, :] & in_range[:, None, :]
    scores = jnp.where(mask[None, :, None, None, :, :], scores, NEG_INF)

    sink_b = sink.astype(jnp.float32).reshape(1, 1, N_KV_HEADS, GQA_GROUP, 1, 1)
    m = jnp.maximum(jnp.max(scores, axis=-1, keepdims=True), sink_b)
    p = jnp.exp(scores - m)
    denom = jnp.sum(p, axis=-1, keepdims=True) + jnp.exp(sink_b - m)
    probs = (p / denom).astype(h.dtype)
    out = jnp.einsum('bnhgqk,bnkhd->bnqhgd', probs, v_win)
    return out.reshape(bsz, seq, q_w) @ w_o


def sqrelu_mlp(h, w_up, w_down):
    return jnp.square(jax.nn.relu(h @ w_up)) @ w_down


def setup_inputs(seed: int = 0) -> dict:
    key = jax.random.key(seed)
    ks = jax.random.split(key, 24)
    f32 = jnp.float32
    nrm = lambda k, shape, s: jax.random.normal(k, shape, f32) * s

    x = jax.random.normal(ks[0], (BATCH, SEQ, D_MODEL), f32)
    offsets = jax.random.randint(ks[1], (BATCH, 1), 0, 1024, jnp.int32)
    positions = (jnp.arange(SEQ, dtype=jnp.int32)[None, :] + offsets).astype(jnp.int32)

    ns, na = N_S5_LAYERS, N_ATTN_LAYERS
    gp = (ns, S5_DIRS, S5_GROUPS, S5_STATE)
    s5_norm = 1.0 + nrm(ks[2], (ns, D_MODEL), 0.02)
    s5_a_re = -0.5 + nrm(ks[3], gp, 0.01)
    s5_a_im = math.pi * jnp.arange(S5_STATE, dtype=f32) + nrm(ks[4], gp, 0.01)
    s5_log_step = jax.random.uniform(ks[5], (ns, S5_DIRS, S5_GROUPS), f32,
                                     math.log(S5_DT_MIN), math.log(S5_DT_MAX))
    bshape = (ns, S5_DIRS, S5_GROUPS, S5_STATE, S5_GROUP)
    s5_b_re = nrm(ks[6], bshape, (2 * S5_GROUP) ** -0.5)
    s5_b_im = nrm(ks[7], bshape, (2 * S5_GROUP) ** -0.5)
    cshape = (ns, S5_DIRS, S5_GROUPS, S5_GROUP, S5_STATE)
    s5_c_re = nrm(ks[8], cshape, S5_STATE ** -0.5)
    s5_c_im = nrm(ks[9], cshape, S5_STATE ** -0.5)
    s5_d = nrm(ks[10], (ns, D_MODEL), 1.0)
    s5_w_glu_a = nrm(ks[11], (ns, D_MODEL, D_MODEL), D_MODEL ** -0.5)
    s5_w_glu_b = nrm(ks[12], (ns, D_MODEL, D_MODEL), D_MODEL ** -0.5)

    qkv_w = (N_Q_HEADS + 2 * N_KV_HEADS) * HEAD_DIM
    attn_norm = 1.0 + nrm(ks[13], (na, D_MODEL), 0.02)
    attn_w_qkv = nrm(ks[14], (na, D_MODEL, qkv_w), D_MODEL ** -0.5)
    attn_q_gain = 1.0 + nrm(ks[15], (na, HEAD_DIM), 0.02)
    attn_k_gain = 1.0 + nrm(ks[16], (na, HEAD_DIM), 0.02)
    attn_sink = nrm(ks[17], (na, N_Q_HEADS), 0.5)
    attn_w_o = nrm(ks[18], (na, N_Q_HEADS * HEAD_DIM, D_MODEL), (N_Q_HEADS * HEAD_DIM) ** -0.5)

    mlp_norm = 1.0 + nrm(ks[19], (DEPTH, D_MODEL), 0.02)
    mlp_w_up = nrm(ks[20], (DEPTH, D_MODEL, D_FF), D_MODEL ** -0.5)
    mlp_w_down = nrm(ks[21], (DEPTH, D_FF, D_MODEL), D_FF ** -0.5)

    return {"x": x, "positions": positions,
            "s5_norm": s5_norm, "s5_a_re": s5_a_re, "s5_a_im": s5_a_im,
            "s5_log_step": s5_log_step, "s5_b_re": s5_b_re, "s5_b_im": s5_b_im,
            "s5_c_re": s5_c_re, "s5_c_im": s5_c_im, "s5_d": s5_d,
            "s5_w_glu_a": s5_w_glu_a, "s5_w_glu_b": s5_w_glu_b,
            "attn_norm": attn_norm, "attn_w_qkv": attn_w_qkv,
            "attn_q_gain": attn_q_gain, "attn_k_gain": attn_k_gain,
            "attn_sink": attn_sink, "attn_w_o": attn_w_o,
            "mlp_norm": mlp_norm, "mlp_w_up": mlp_w_up, "mlp_w_down": mlp_w_down}


def reference(x, positions, s5_norm, s5_a_re, s5_a_im, s5_log_step, s5_b_re, s5_b_im,
              s5_c_re, s5_c_im, s5_d, s5_w_glu_a, s5_w_glu_b, attn_norm, attn_w_qkv,
              attn_q_gain, attn_k_gain, attn_sink, attn_w_o, mlp_norm, mlp_w_up,
              mlp_w_down):
    h = x
    for i in range(DEPTH):
        j = i // N_MIXERS
        if i % N_MIXERS == 0:
            h = h + s5_mixer(rms_norm(h, s5_norm[j]), s5_a_re[j], s5_a_im[j], s5_log_step[j],
                             s5_b_re[j], s5_b_im[j], s5_c_re[j], s5_c_im[j], s5_d[j],
                             s5_w_glu_a[j], s5_w_glu_b[j])
        else:
            h = h + windowed_gqa(rms_norm(h, attn_norm[j]), positions, attn_w_qkv[j],
                                 attn_q_gain[j], attn_k_gain[j], attn_sink[j], attn_w_o[j])
        h = h + sqrelu_mlp(rms_norm(h, mlp_norm[i]), mlp_w_up[i], mlp_w_down[i])
    return h
```

```python
import numpy as np
import concourse.bass as bass
import concourse.mybir as mybir
from concourse.bass_utils import run_bass_kernel_spmd

F32 = mybir.dt.float32
BF16 = mybir.dt.bfloat16
I32 = mybir.dt.int32
U8 = mybir.dt.uint8
AF = mybir.ActivationFunctionType
ALU = mybir.AluOpType
AX = mybir.AxisListType

D = 2048
L = 4096
DFF = 8192
NCORES = 8
EPS = 1e-6


class Op:
    __slots__ = ("eng", "fn", "deps", "marked", "incval", "is_dma", "sem", "semval", "prev_on_sem", "idx")


class Prog:
    ENGS = ("pe", "act", "dve", "pool", "sp")

    def __init__(self, nc, n_dma_sems):
        self.nc = nc
        self.ops = {e: [] for e in self.ENGS}
        self.lastw = {}
        self.readers = {}
        self.n_dma_sems = n_dma_sems
        self.dma_rr = {q: 0 for q in n_dma_sems}
        self.dma_last = {}
        self.dma_cnt = {}
        self.nops = 0

    def _deps(self, op, reads, writes):
        deps = []
        for r in reads:
            w = self.lastw.get(r)
            if w is not None:
                deps.append(w)
        for r in writes:
            w = self.lastw.get(r)
            if w is not None:
                deps.append(w)
            deps.extend(self.readers.get(r, ()))
        for r in writes:
            self.lastw[r] = op
            self.readers[r] = []
        for r in reads:
            self.readers.setdefault(r, []).append(op)
        return deps

    def op(self, eng, fn, reads=(), writes=()):
        o = Op()
        o.eng = eng
        o.fn = fn
        o.is_dma = False
        o.marked = False
        o.incval = 0
        o.idx = self.nops
        self.nops += 1
        deps = self._deps(o, reads, writes)
        o.deps = [d for d in deps if not (eng == "pe" and (not d.is_dma) and d.eng == "pe")]
        self.ops[eng].append(o)
        return o

    def dma(self, queue, out, in_, reads=(), writes=(), **kw):
        o = Op()
        o.eng = queue
        o.is_dma = True
        o.marked = True
        o.incval = 0
        o.idx = self.nops
        self.nops += 1
        k = self.dma_rr[queue]
        self.dma_rr[queue] = (k + 1) % self.n_dma_sems[queue]
        o.sem = (queue, k)
        o.prev_on_sem = self.dma_last.get(o.sem)
        self.dma_last[o.sem] = o
        self.dma_cnt[o.sem] = self.dma_cnt.get(o.sem, 0) + 16
        o.semval = self.dma_cnt[o.sem]
        o.fn = lambda e: e.dma_start(out=out, in_=in_, **kw)
        o.deps = self._deps(o, reads, writes)
        if o.prev_on_sem is not None:
            o.deps.append(o.prev_on_sem)
        self.ops[queue].append(o)
        return o

    def barrier(self):
        deps = []
        for e in self.ENGS:
            for o in reversed(self.ops[e]):
                if not o.is_dma:
                    deps.append(o)
                    break
        deps += list(self.dma_last.values())
        for e in self.ENGS:
            o = Op()
            o.eng = e
            o.fn = lambda eng: eng.nop()
            o.is_dma = False
            o.marked = False
            o.incval = 0
            o.idx = self.nops
            self.nops += 1
            o.deps = list(deps)
            self.ops[e].append(o)

    def emit(self, final_waits=()):
        nc = self.nc
        import contextlib
        with contextlib.ExitStack() as st:
            esem = {e: st.enter_context(nc.semaphore("sem_" + e)) for e in self.ENGS}
            dsem = {}
            for q, n in self.n_dma_sems.items():
                for k in range(n):
                    dsem[(q, k)] = st.enter_context(nc.semaphore("dsem_%s_%d" % (q, k)))
            for e in self.ENGS:
                for o in self.ops[e]:
                    for d in o.deps:
                        d.marked = True
            for e in self.ENGS:
                c = 0
                for o in self.ops[e]:
                    if o.is_dma:
                        continue
                    if o.marked:
                        c += 1
                        o.incval = c
            block = st.enter_context(nc.Block())
            handles = {"pe": block.tensor, "act": block.scalar, "dve": block.vector, "pool": block.gpsimd, "sp": block.sync}

            def make(e):
                def body(eng):
                    waited = {}
                    for o in self.ops[e]:
                        need = {}
                        for d in o.deps:
                            if d.is_dma:
                                s, v = dsem[d.sem], d.semval
                                key = ("d",) + d.sem
                            else:
                                s, v = esem[d.eng], d.incval
                                key = ("e", d.eng)
                            if waited.get(key, 0) >= v:
                                continue
                            if key not in need or need[key][1] < v:
                                need[key] = (s, v)
                        for key, (s, v) in need.items():
                            eng.wait_ge(s, v)
                            waited[key] = v
                        ins = o.fn(eng)
                        if o.is_dma:
                            ins.then_inc(dsem[o.sem], 16)
                        elif o.marked:
                            ins.then_inc(esem[e], 1)
                    if e == "sp":
                        for key, v in self.dma_cnt.items():
                            eng.wait_ge(dsem[key], v)
                return body

            for e in self.ENGS:
                handles[e](make(e))


class Arena:
    def __init__(self, big_ap, size):
        self.big = big_ap
        self.size = size
        self.off = 0
        self.marks = []

    def alloc(self, nbytes, dtype, parts=128):
        nbytes = (nbytes + 63) // 64 * 64
        assert self.off + nbytes <= self.size, ("SBUF arena overflow", self.off, nbytes, self.size)
        ap = self.big[0:parts, self.off:self.off + nbytes].bitcast(dtype)
        self.off += nbytes
        return ap

    def push(self):
        self.marks.append(self.off)

    def pop(self):
        self.off = self.marks.pop()


def hres(i):
    return ["h%dc%d" % (i, cb) for cb in range(4)]


def f32(arena, n, parts=128):
    return arena.alloc(4 * n, F32, parts)


def b16(arena, n, parts=128):
    return arena.alloc(2 * n, BF16, parts)


class _Stop(Exception):
    pass


class Builder:
    def __init__(self, nc, cfg):
        self.nc = nc
        self.cfg = cfg
        self.P = Prog(nc, {"sp": 24, "pool": 12, "act": 6})
        self.uid = 0

    def u(self, s):
        self.uid += 1
        return "%s#%d" % (s, self.uid)

    def precast_mlp(self, l):
        P = self.P
        T = self.T
        wup = T["mlp_w_up"][l]
        wdn = T["mlp_w_down"][l]
        wup_s = T["wup_s"][l]
        wdn_s = T["wdn_s"][l]
        for ub in range(32):
            src = wup[:, ub * 256:(ub + 1) * 256].rearrange("(kt p) c -> p kt c", p=128)
            P.dma("pool", wup_s[ub], src, writes=["wup_s%d_%d" % (l, ub)])
        for cb in range(4):
            for ch in range(8):
                src = wdn[ch * 1024:(ch + 1) * 1024, cb * 512:(cb + 1) * 512].rearrange("(mt p) c -> p mt c", p=128)
                P.dma("pool", wdn_s[cb, ch], src, writes=["wdn_s%d_%d_%d" % (l, cb, ch)])

    def mlp_phase(self, l, src_h, dst_h, A):
        P = self.P
        T = self.T
        ps = self.ps
        psb = self.psb
        ident_b = self.ident_b
        A.push()
        gain = f32(A, D)
        hst = [f32(A, D) for _ in range(2)]
        xn = [b16(A, D) for _ in range(4)]
        junk = f32(A, D)
        stat = f32(A, 16)
        xT = b16(A, 16 * 512)
        hT = b16(A, 64 * 512)
        wu = [b16(A, 16 * 256) for _ in range(3)]
        wd = [b16(A, 8 * 512) for _ in range(3)]
        rst = [f32(A, 512) for _ in range(4)]
        tmp = [f32(A, 512) for _ in range(2)]
        xTv = xT.rearrange("p (k t) -> p k t", k=16)
        hTv = hT.rearrange("p (m t) -> p m t", m=64)
        nm = "mlp%d" % l
        P.dma("sp", gain, T["mlp_norm"][l:l + 1, :].partition_broadcast(128), writes=[nm + "gain"])
        wup_s = T["wup_s"][l]
        wdn_s = T["wdn_s"][l]
        n_st = self.cfg.get("n_st", 8)
        cnt = {"wu": 0, "wd": 0, "hs": 0, "rs": 0, "tp": 0}

        def norm_stage(st):
            for ts in range(4):
                i = st * 4 + ts
                s = cnt["hs"] % 2
                cnt["hs"] += 1
                P.dma("sp", hst[s], src_h[i * 128:(i + 1) * 128, :], reads=hres(i), writes=[nm + "hst%d" % s])
                P.op("act", lambda e, s=s, ts=ts: e.activation(out=junk, in_=hst[s], func=AF.Square, accum_out=stat[:, ts:ts + 1]),
                     reads=[nm + "hst%d" % s], writes=[nm + "junk", nm + "sq%d" % ts])
                P.op("dve", lambda e, ts=ts: e.tensor_scalar(out=stat[:, 4 + ts:5 + ts], in0=stat[:, ts:ts + 1], scalar1=1.0 / D, scalar2=EPS, op0=ALU.mult, op1=ALU.add),
                     reads=[nm + "sq%d" % ts], writes=[nm + "ms%d" % ts])
                P.op("act", lambda e, ts=ts: e.activation(out=stat[:, 8 + ts:9 + ts], in_=stat[:, 4 + ts:5 + ts], func=AF.Sqrt),
                     reads=[nm + "ms%d" % ts], writes=[nm + "sd%d" % ts])
                P.op("dve", lambda e, ts=ts: e.reciprocal(out=stat[:, 12 + ts:13 + ts], in_=stat[:, 8 + ts:9 + ts]),
                     reads=[nm + "sd%d" % ts], writes=[nm + "rstd%d" % ts])
                P.op("dve", lambda e, s=s, ts=ts: e.scalar_tensor_tensor(out=xn[ts], in0=hst[s], scalar=stat[:, 12 + ts:13 + ts], in1=gain, op0=ALU.mult, op1=ALU.mult),
                     reads=[nm + "hst%d" % s, nm + "rstd%d" % ts, nm + "gain"], writes=[nm + "xn%d" % ts])

        def tr_stage(st):
            for ts in range(4):
                for kg in range(4):
                    pb = cnt["tp"] % 2
                    cnt["tp"] += 1
                    for kk in range(4):
                        k = kg * 4 + kk
                        P.op("pe", lambda e, ts=ts, k=k, kk=kk, pb=pb: e.transpose(psb[pb][:, kk * 128:(kk + 1) * 128], xn[ts][:, k * 128:(k + 1) * 128], ident_b),
                             reads=[nm + "xn%d" % ts, "ident_b"], writes=["ps%d" % (2 + pb)])
                    dst = xTv[:, kg * 4:(kg + 1) * 4, ts * 128:(ts + 1) * 128]
                    srcp = psb[pb][:, 0:512].rearrange("p (k t) -> p k t", k=4)
                    if kg % 2 == 0:
                        P.op("act", lambda e, dst=dst, srcp=srcp: e.copy(out=dst, in_=srcp),
                             reads=["ps%d" % (2 + pb)], writes=[nm + "xT%d_%d" % (kg, ts)])
                    else:
                        P.op("dve", lambda e, dst=dst, srcp=srcp: e.tensor_copy(out=dst, in_=srcp),
                             reads=["ps%d" % (2 + pb)], writes=[nm + "xT%d_%d" % (kg, ts)])

        xT_res = [nm + "xT%d_%d" % (kg, ts) for kg in range(4) for ts in range(4)]

        def up_stage(st):
            for ub in range(32):
                s = cnt["wu"] % 3
                cnt["wu"] += 1
                P.dma("sp", wu[s], wup_s[ub].rearrange("p k c -> p (k c)"), reads=["wup_s%d_%d" % (l, ub)], writes=[nm + "wu%d" % s])
                wuv = wu[s].rearrange("p (k c) -> p k c", k=16)
                for c2 in range(2):
                    m = ub * 2 + c2
                    bank = m % 4
                    for k in range(16):
                        P.op("pe", lambda e, wuv=wuv, k=k, c2=c2, bank=bank: e.matmul(ps[bank], lhsT=wuv[:, k, c2 * 128:(c2 + 1) * 128], rhs=xTv[:, k, :], start=(k == 0), stop=(k == 15)),
                             reads=[nm + "wu%d" % s] + (xT_res if k in (0, 15) else []), writes=["ps%d" % bank])
                    t = m % 2
                    P.op("act", lambda e, t=t, bank=bank: e.activation(out=tmp[t], in_=ps[bank], func=AF.Relu),
                         reads=["ps%d" % bank], writes=[nm + "tmp%d" % t])
                    P.op("dve", lambda e, t=t, m=m: e.tensor_tensor(out=hTv[:, m, :], in0=tmp[t], in1=tmp[t], op=ALU.mult),
                         reads=[nm + "tmp%d" % t], writes=[nm + "hT%d" % m])

        def down_stage(st):
            for cb in range(4):
                for ch in range(8):
                    s = cnt["wd"] % 3
                    cnt["wd"] += 1
                    P.dma("sp", wd[s], wdn_s[cb, ch].rearrange("p m c -> p (m c)"), reads=["wdn_s%d_%d_%d" % (l, cb, ch)], writes=[nm + "wd%d" % s])
                    wdv = wd[s].rearrange("p (m c) -> p m c", m=8)
                    for ts in range(4):
                        for mt in range(8):
                            m = ch * 8 + mt
                            P.op("pe", lambda e, wdv=wdv, m=m, mt=mt, ts=ts: e.matmul(ps[4 + ts], lhsT=hTv[:, m, ts * 128:(ts + 1) * 128], rhs=wdv[:, mt, :], start=(m == 0), stop=(m == 63)),
                                 reads=[nm + "wd%d" % s, nm + "hT%d" % m], writes=["ps%d" % (4 + ts)])
                for ts in range(4):
                    i = st * 4 + ts
                    s = cnt["rs"] % 4
                    cnt["rs"] += 1
                    P.dma("act", rst[s], src_h[i * 128:(i + 1) * 128, cb * 512:(cb + 1) * 512], reads=[hres(i)[cb]], writes=[nm + "rst%d" % s])
                    P.op("dve", lambda e, s=s, ts=ts: e.tensor_tensor(out=rst[s], in0=ps[4 + ts], in1=rst[s], op=ALU.add),
                         reads=["ps%d" % (4 + ts), nm + "rst%d" % s], writes=[nm + "rst%d" % s])
                    P.dma("act", dst_h[i * 128:(i + 1) * 128, cb * 512:(cb + 1) * 512], rst[s], reads=[nm + "rst%d" % s], writes=[hres(i)[cb]])

        norm_stage(0)
        for st in range(n_st):
            tr_stage(st)
            up_stage(st)
            if st + 1 < n_st:
                norm_stage(st + 1)
            down_stage(st)
        A.pop()

    def norm_tile(self, nm, src_h, i, gain, hst, xn, junkb, stat, xTt, c, rows=None, rres=None):
        P = self.P
        psb = self.psb
        s = c["n"] % 2
        c["n"] += 1
        if rows is None:
            rows = src_h[i * 128:(i + 1) * 128, :]
            rres = hres(i)
        P.dma("sp", hst[s], rows, reads=rres, writes=[nm + "hst%d" % s])
        P.op("act", lambda e: e.activation(out=junkb, in_=hst[s], func=AF.Square, accum_out=stat[:, 0:1]),
             reads=[nm + "hst%d" % s], writes=[nm + "junk", nm + "sq"])
        P.op("dve", lambda e: e.tensor_scalar(out=stat[:, 1:2], in0=stat[:, 0:1], scalar1=1.0 / D, scalar2=EPS, op0=ALU.mult, op1=ALU.add),
             reads=[nm + "sq"], writes=[nm + "ms"])
        P.op("act", lambda e: e.activation(out=stat[:, 2:3], in_=stat[:, 1:2], func=AF.Sqrt), reads=[nm + "ms"], writes=[nm + "sd"])
        P.op("dve", lambda e: e.reciprocal(out=stat[:, 3:4], in_=stat[:, 2:3]), reads=[nm + "sd"], writes=[nm + "rstd"])
        P.op("dve", lambda e: e.scalar_tensor_tensor(out=xn[s], in0=hst[s], scalar=stat[:, 3:4], in1=gain, op0=ALU.mult, op1=ALU.mult),
             reads=[nm + "hst%d" % s, nm + "rstd", nm + "gain"], writes=[nm + "xn%d" % s])
        if xTt is None:
            return s
        for kg in range(4):
            pb = c["tp"] % 2
            c["tp"] += 1
            for kk in range(4):
                k = kg * 4 + kk
                P.op("pe", lambda e, k=k, kk=kk, pb=pb: e.transpose(psb[pb][:, kk * 128:(kk + 1) * 128], xn[s][:, k * 128:(k + 1) * 128], self.ident_b),
                     reads=[nm + "xn%d" % s, "ident_b"], writes=["ps%d" % (2 + pb)])
            dst = xTt[s][:, kg * 4:(kg + 1) * 4, :]
            srcp = psb[pb][:, 0:512].rearrange("p (k t) -> p k t", k=4)
            if kg % 2 == 0:
                P.op("act", lambda e, dst=dst, srcp=srcp: e.copy(out=dst, in_=srcp), reads=["ps%d" % (2 + pb)], writes=[nm + "xT%d_%d" % (s, kg)])
            else:
                P.op("dve", lambda e, dst=dst, srcp=srcp: e.tensor_copy(out=dst, in_=srcp), reads=["ps%d" % (2 + pb)], writes=[nm + "xT%d_%d" % (s, kg)])
        return s

    def turns_to_sin(self, key, y, dst, ti, tf, tm, rd, wr):
        import math
        P = self.P
        P.op("dve", lambda e: e.tensor_copy(out=ti, in_=y), reads=rd, writes=[key + "ti"])
        P.op("dve", lambda e: e.tensor_copy(out=tf, in_=ti), reads=[key + "ti"], writes=[key + "tf"])
        P.op("dve", lambda e: e.tensor_tensor(out=tf, in0=y, in1=tf, op=ALU.subtract), reads=rd + [key + "tf"], writes=[key + "tf"])
        P.op("dve", lambda e: e.tensor_scalar(out=tm, in0=tf, scalar1=0.5, scalar2=None, op0=ALU.is_gt), reads=[key + "tf"], writes=[key + "tm"])
        P.op("dve", lambda e: e.tensor_tensor(out=tf, in0=tf, in1=tm, op=ALU.subtract), reads=[key + "tf", key + "tm"], writes=[key + "tf"])
        P.op("act", lambda e: e.activation(out=dst, in_=tf, func=AF.Sin, scale=2.0 * math.pi), reads=[key + "tf"], writes=wr)

    def precast_attn(self, j):
        P, T = self.P, self.T
        src = T["attn_w_qkv"][j].rearrange("(kt p) c -> p kt c", p=128)
        for kt0 in range(0, 16, 4):
            P.dma("pool", T["wqkv_s"][j][:, kt0:kt0 + 4, :], src[:, kt0:kt0 + 4, :], writes=["wqkv_s%d_%d" % (j, kt0)])
        src = T["attn_w_o"][j].rearrange("(kt p) c -> p kt c", p=128)
        for kt0 in range(0, 16, 4):
            P.dma("pool", T["wo_s"][j][:, kt0:kt0 + 4, :], src[:, kt0:kt0 + 4, :], writes=["wo_s%d_%d" % (j, kt0)])

    def attn_phase(self, j, h, A):
        import math
        P, T, ps, psb = self.P, self.T, self.ps, self.psb
        ident_b = self.ident_b
        nm = "at%d" % j
        qT_s, kT_s, v_s = T["qT_s"], T["kT_s"], T["v_s"]
        NT = self.cfg.get("n_tok_tiles", 32)
        A.push()
        A.push()
        gain = f32(A, D)
        hst = [f32(A, D) for _ in range(2)]
        xn = [b16(A, D) for _ in range(2)]
        junkb = b16(A, D)
        stat = f32(A, 8)
        xTt = [b16(A, 16 * 128).rearrange("p (k t) -> p k t", k=16) for _ in range(2)]
        wq = b16(A, 16 * 2560).rearrange("p (k c) -> p k c", k=16)
        g8 = f32(A, 1024)
        posi = A.alloc(4 * 32, I32)
        posf = f32(A, 32)
        invf = f32(A, 32)
        ang = f32(A, 1024)
        tw = f32(A, 1024)
        cs2 = f32(A, 32 * 64)
        sn2 = f32(A, 32 * 64)
        t32 = [f32(A, 512) for _ in range(2)]
        sq = f32(A, 512)
        hs = f32(A, 32)
        qn = f32(A, 512)
        tA = f32(A, 512)
        tB = f32(A, 512)
        qb16 = b16(A, 2048)
        kb16 = b16(A, 512)
        vb16 = [b16(A, 256) for _ in range(2)]
        qTt = b16(A, 16 * 512).rearrange("p (j t) -> p j t", j=16)
        kTt = b16(A, 4 * 512).rearrange("p (j t) -> p j t", j=4)
        P.dma("sp", gain, T["attn_norm"][j:j + 1, :].partition_broadcast(128), writes=[nm + "gain"])
        for kt0 in range(0, 16, 4):
            P.dma("sp", wq[:, kt0:kt0 + 4, :], T["wqkv_s"][j][:, kt0:kt0 + 4, :], reads=["wqkv_s%d_%d" % (j, kt0)], writes=[nm + "wq%d" % kt0])
        wq_res = [nm + "wq%d" % k for k in range(0, 16, 4)]
        P.dma("sp", g8[:, 0:64], T["attn_q_gain"][j:j + 1, :].partition_broadcast(128), writes=[nm + "g0"])
        P.dma("sp", g8[:, 512:576], T["attn_k_gain"][j:j + 1, :].partition_broadcast(128), writes=[nm + "g1"])
        P.op("dve", lambda e: e.tensor_copy(out=g8[:, 64:512].rearrange("p (h d) -> p h d", d=64), in_=g8[:, 0:64].unsqueeze(1).broadcast_to([128, 7, 64])),
             reads=[nm + "g0"], writes=[nm + "g0b"])
        P.op("dve", lambda e: e.tensor_copy(out=g8[:, 576:768].rearrange("p (h d) -> p h d", d=64), in_=g8[:, 512:576].unsqueeze(1).broadcast_to([128, 3, 64])),
             reads=[nm + "g1"], writes=[nm + "g1b"])
        P.dma("sp", posi, T["positions"].rearrange("(t p) -> p t", p=128), writes=[nm + "posi"], allow_slow_non_contiguous=True)
        P.op("dve", lambda e: e.tensor_copy(out=posf, in_=posi), reads=[nm + "posi"], writes=[nm + "posf"])
        for i in range(32):
            val = float(np.float32(10000.0) ** np.float32(-(2.0 * i) / 64.0)) / (2.0 * math.pi)
            P.op("pool", lambda e, i=i, val=val: e.memset(invf[:, i:i + 1], val), writes=[nm + "invf"])
        angv = ang.rearrange("p (t f) -> p t f", f=32)
        P.op("dve", lambda e: e.tensor_tensor(out=angv, in0=posf.unsqueeze(2).broadcast_to([128, 32, 32]), in1=invf.unsqueeze(1).broadcast_to([128, 32, 32]), op=ALU.mult),
             reads=[nm + "posf", nm + "invf"], writes=[nm + "ang"])
        cs2v = cs2.rearrange("p (t f) -> p t f", f=64)
        sn2v = sn2.rearrange("p (t f) -> p t f", f=64)
        ti = tw.bitcast(I32)
        tfw = f32(A, 1024)
        tmw = f32(A, 1024)
        t3 = lambda ap: ap.rearrange("p (t f) -> p t f", f=32)
        self.turns_to_sin(nm + "sn", angv, sn2v[:, :, 0:32], t3(ti), t3(tfw), t3(tmw), [nm + "ang"], [nm + "sna", nm + "snb"])
        P.op("dve", lambda e: e.tensor_scalar(out=ang, in0=ang, scalar1=0.25, scalar2=None, op0=ALU.add), reads=[nm + "ang", nm + "snti", nm + "sntf"], writes=[nm + "ang"])
        self.turns_to_sin(nm + "cs", angv, cs2v[:, :, 0:32], t3(ti), t3(tfw), t3(tmw), [nm + "ang"], [nm + "csa", nm + "csb"])
        P.op("dve", lambda e: e.tensor_copy(out=cs2v[:, :, 32:64], in_=cs2v[:, :, 0:32]), reads=[nm + "csa", nm + "csb"], writes=[nm + "cs2"])
        P.op("dve", lambda e: e.tensor_copy(out=sn2v[:, :, 32:64], in_=sn2v[:, :, 0:32]), reads=[nm + "sna", nm + "snb"], writes=[nm + "sn2h"])
        P.op("dve", lambda e: e.tensor_scalar(out=sn2v[:, :, 0:32], in0=sn2v[:, :, 0:32], scalar1=-1.0, scalar2=None, op0=ALU.mult), reads=[nm + "sn2h"], writes=[nm + "sn2"])
        c = {"n": 0, "tp": 0}
        vi = 0
        for i in range(NT):
            s = self.norm_tile(nm, h, i, gain, hst, xn, junkb, stat, xTt, c)
            xres = [nm + "xT%d_%d" % (s, kg) for kg in range(4)]
            ts = i % 4
            for cb in range(5):
                bank = cb % 2
                for k in range(16):
                    P.op("pe", lambda e, k=k, cb=cb, bank=bank, s=s: e.matmul(ps[bank], lhsT=xTt[s][:, k, :], rhs=wq[:, k, cb * 512:(cb + 1) * 512], start=(k == 0), stop=(k == 15)),
                         reads=(xres + wq_res if k in (0, 15) else []), writes=["ps%d" % bank])
                t = t32[cb % 2]
                tn = nm + "t32_%d" % (cb % 2)
                nh = 8 if cb < 4 else 4
                w = nh * 64
                P.op("act", lambda e, t=t, bank=bank: e.copy(out=t, in_=ps[bank]), reads=["ps%d" % bank], writes=[tn])
                if cb == 4:
                    vb = vb16[vi % 2]
                    vn = nm + "vb%d" % (vi % 2)
                    vi += 1
                    P.op("pool", lambda e, t=t, vb=vb: e.tensor_copy(out=vb, in_=t[:, 256:512]), reads=[tn], writes=[vn])
                    P.dma("act", v_s[i * 128:(i + 1) * 128, :], vb, reads=[vn], writes=[nm + "v_s%d" % i])
                gofs = 0 if cb < 4 else 512
                P.op("dve", lambda e, t=t, w=w: e.tensor_tensor(out=sq[:, 0:w], in0=t[:, 0:w], in1=t[:, 0:w], op=ALU.mult), reads=[tn], writes=[nm + "sq"])
                P.op("dve", lambda e, nh=nh, w=w: e.tensor_reduce(out=hs[:, 0:nh], in_=sq[:, 0:w].rearrange("p (h d) -> p h d", d=64), axis=AX.X, op=ALU.add),
                     reads=[nm + "sq"], writes=[nm + "hs0"])
                P.op("dve", lambda e, nh=nh: e.tensor_scalar(out=hs[:, 8:8 + nh], in0=hs[:, 0:nh], scalar1=1.0 / 64, scalar2=EPS, op0=ALU.mult, op1=ALU.add),
                     reads=[nm + "hs0"], writes=[nm + "hs1"])
                P.op("act", lambda e, nh=nh: e.activation(out=hs[:, 16:16 + nh], in_=hs[:, 8:8 + nh], func=AF.Sqrt), reads=[nm + "hs1"], writes=[nm + "hs2"])
                P.op("dve", lambda e, nh=nh: e.reciprocal(out=hs[:, 24:24 + nh], in_=hs[:, 16:16 + nh]), reads=[nm + "hs2"], writes=[nm + "hs3"])
                v3 = lambda ap, w=w: ap[:, 0:w].rearrange("p (h d) -> p h d", d=64)
                P.op("dve", lambda e, t=t, nh=nh, v3=v3: e.tensor_tensor(out=v3(qn), in0=v3(t), in1=hs[:, 24:24 + nh].unsqueeze(2).broadcast_to([128, nh, 64]), op=ALU.mult),
                     reads=[tn, nm + "hs3"], writes=[nm + "qn"])
                P.op("dve", lambda e, w=w, gofs=gofs: e.tensor_tensor(out=qn[:, 0:w], in0=qn[:, 0:w], in1=g8[:, gofs:gofs + w], op=ALU.mult),
                     reads=[nm + "qn", nm + "g0b", nm + "g1b", nm + "g0", nm + "g1"], writes=[nm + "qn"])
                P.op("dve", lambda e, nh=nh, v3=v3, i=i: e.tensor_tensor(out=v3(tA), in0=v3(qn), in1=cs2v[:, i:i + 1, :].broadcast_to([128, nh, 64]), op=ALU.mult),
                     reads=[nm + "qn", nm + "cs2"], writes=[nm + "tA"])
                P.op("dve", lambda e, nh=nh, v3=v3, i=i: e.tensor_tensor(out=v3(tB)[:, :, 0:32], in0=v3(qn)[:, :, 32:64], in1=sn2v[:, i:i + 1, 0:32].broadcast_to([128, nh, 32]), op=ALU.mult),
                     reads=[nm + "qn", nm + "sn2"], writes=[nm + "tB0"])
                P.op("dve", lambda e, nh=nh, v3=v3, i=i: e.tensor_tensor(out=v3(tB)[:, :, 32:64], in0=v3(qn)[:, :, 0:32], in1=sn2v[:, i:i + 1, 32:64].broadcast_to([128, nh, 32]), op=ALU.mult),
                     reads=[nm + "qn", nm + "sn2"], writes=[nm + "tB1"])
                if cb < 4:
                    P.op("dve", lambda e, cb=cb: e.tensor_tensor(out=qb16[:, cb * 512:(cb + 1) * 512], in0=tA, in1=tB, op=ALU.add),
                         reads=[nm + "tA", nm + "tB0", nm + "tB1"], writes=[nm + "qb%d" % cb])
                else:
                    kdst = kb16.rearrange("p (h u d) -> p h u d", h=4, u=2)
                    for u in range(2):
                        P.op("dve", lambda e, u=u, kdst=kdst: e.tensor_tensor(out=kdst[:, :, u, :], in0=tA[:, 0:256].rearrange("p (h d) -> p h d", d=64), in1=tB[:, 0:256].rearrange("p (h d) -> p h d", d=64), op=ALU.add),
                             reads=[nm + "tA", nm + "tB0", nm + "tB1"], writes=[nm + "kb%d" % u])
            for kg in range(5):
                pb = c["tp"] % 2
                c["tp"] += 1
                for kk in range(4):
                    if kg < 4:
                        src_ap = qb16[:, (kg * 4 + kk) * 128:(kg * 4 + kk + 1) * 128]
                        rd = [nm + "qb%d" % kg]
                    else:
                        src_ap = kb16[:, kk * 128:(kk + 1) * 128]
                        rd = [nm + "kb0", nm + "kb1"]
                    P.op("pe", lambda e, src_ap=src_ap, kk=kk, pb=pb: e.transpose(psb[pb][:, kk * 128:(kk + 1) * 128], src_ap, ident_b),
                         reads=rd + ["ident_b"], writes=["ps%d" % (2 + pb)])
                srcp = psb[pb][:, 0:512].rearrange("p (k t) -> p k t", k=4)
                if kg < 4:
                    dst = qTt[:, kg * 4:(kg + 1) * 4, ts * 128:(ts + 1) * 128]
                    wr = [nm + "qTt%d_%d" % (kg, ts)]
                else:
                    dst = kTt[:, :, ts * 128:(ts + 1) * 128]
                    wr = [nm + "kTt%d" % ts]
                if kg % 2 == 0:
                    P.op("act", lambda e, dst=dst, srcp=srcp: e.copy(out=dst, in_=srcp), reads=["ps%d" % (2 + pb)], writes=wr)
                else:
                    P.op("dve", lambda e, dst=dst, srcp=srcp: e.tensor_copy(out=dst, in_=srcp), reads=["ps%d" % (2 + pb)], writes=wr)
            if ts == 3:
                st = i // 4
                P.dma("act", qT_s[:, :, st * 512:(st + 1) * 512].rearrange("j p t -> p j t"), qTt,
                      reads=[nm + "qTt%d_%d" % (kg, t4) for kg in range(4) for t4 in range(4)], writes=[nm + "qT_s%d" % st])
                P.dma("act", kT_s[:, :, st * 512:(st + 1) * 512].rearrange("j p t -> p j t"), kTt,
                      reads=[nm + "kTt%d" % t4 for t4 in range(4)], writes=[nm + "kT_s%d" % st])
        A.pop()
        P.barrier()
        A.push()
        NB = NT
        wo = b16(A, 16 * 2048).rearrange("p (k c) -> p k c", k=16)
        kT = b16(A, 4 * 4096).rearrange("p (j t) -> p j t", j=4)
        vall = b16(A, 32 * 256).rearrange("p (b c) -> p b c", b=32)
        qTb = [b16(A, 16 * 128).rearrange("p (j t) -> p j t", j=16) for _ in range(3)]
        vp = [b16(A, 4 * 2 * 128).rearrange("p (h u c) -> p h u c", h=4, u=2) for _ in range(4)]
        onesp = b16(A, 2 * 128).rearrange("p (u c) -> p u c", u=2)
        pT = [[b16(A, 512) for _ in range(6)] for _ in range(2)]
        aT = [b16(A, 16 * 128).rearrange("p (j t) -> p j t", j=16) for _ in range(2)]
        den = [f32(A, 512) for _ in range(2)]
        rst = [f32(A, 512) for _ in range(4)]
        mb = [b16(A, 512) for _ in range(2)]
        mbf = f32(A, 512)
        sk = f32(A, 16)
        ske = f32(A, 16)
        for kt0 in range(0, 16, 4):
            P.dma("sp", wo[:, kt0:kt0 + 4, :], T["wo_s"][j][:, kt0:kt0 + 4, :], reads=["wo_s%d_%d" % (j, kt0)], writes=[nm + "wo%d" % kt0])
        wo_res = [nm + "wo%d" % k for k in range(0, 16, 4)]
        nst = (NT + 3) // 4
        for st in range(nst):
            P.dma("sp", kT[:, :, st * 512:(st + 1) * 512], kT_s[:, :, st * 512:(st + 1) * 512].rearrange("j p t -> p j t"), reads=[nm + "kT_s%d" % st], writes=[nm + "kT%d" % st])
        P.dma("sp", vall[:, 0:NT, :], v_s[0:NT * 128, :].rearrange("(b p) c -> p b c", p=128), reads=[nm + "v_s%d" % i for i in range(NT)], writes=[nm + "vall"])
        sv = T["attn_sink"][j, :].rearrange("(j two) -> two j", two=2)
        P.dma("sp", sk[0:64, :], sv[0:1, :].partition_broadcast(64), writes=[nm + "sk0"], allow_slow_non_contiguous=True)
        P.dma("sp", sk[64:128, :], sv[1:2, :].partition_broadcast(64), writes=[nm + "sk1"], allow_slow_non_contiguous=True)
        P.op("act", lambda e: e.activation(out=ske, in_=sk, func=AF.Exp), reads=[nm + "sk0", nm + "sk1"], writes=[nm + "ske"])
        NEG = -30000.0
        for mi, cm in ((0, 1), (1, -1)):
            P.op("pool", lambda e: e.memset(mbf, 0.0), reads=[], writes=[nm + "mbf"])
            P.op("pool", lambda e, cm=cm: e.affine_select(out=mbf.rearrange("p (h q) -> p h q", h=4), in_=mbf.rearrange("p (h q) -> p h q", h=4), pattern=[[0, 4], [-cm, 128]], compare_op=ALU.is_ge, fill=NEG, base=0, channel_multiplier=cm),
                 reads=[nm + "mbf"], writes=[nm + "mbf"])
            P.op("pool", lambda e, mi=mi: e.tensor_copy(out=mb[mi], in_=mbf), reads=[nm + "mbf"], writes=[nm + "mb%d" % mi])
        for sl in range(4):
            P.op("pool", lambda e, sl=sl: e.memset(vp[sl].rearrange("p h u c -> p (h u c)"), 0.0), writes=[nm + "vp%d" % sl])
        P.op("pool", lambda e: e.memset(onesp.rearrange("p u c -> p (u c)"), 0.0), writes=[nm + "ones"])
        P.op("pool", lambda e: e.memset(onesp[:, 0, 0:64], 1.0), reads=[nm + "ones"], writes=[nm + "ones"])
        P.op("pool", lambda e: e.memset(onesp[:, 1, 64:128], 1.0), reads=[nm + "ones"], writes=[nm + "ones"])

        def load_v(kb):
            sl = kb % 4
            P.op("pool", lambda e, sl=sl, kb=kb: e.tensor_copy(out=vp[sl][:, :, 0, 0:64], in_=vall[:, kb, :].rearrange("p (h d) -> p h d", d=64)),
                 reads=[nm + "vall"], writes=[nm + "vp%d" % sl])
            P.op("pool", lambda e, sl=sl, kb=kb: e.tensor_copy(out=vp[sl][:, :, 1, 64:128], in_=vall[:, kb, :].rearrange("p (h d) -> p h d", d=64)),
                 reads=[nm + "vp%d" % sl, nm + "vall"], writes=[nm + "vp%d" % sl])

        load_v(0)
        rs_i = 0
        grp = 0
        for qb in range(NB):
            if qb + 1 < NB:
                load_v(qb + 1)
            qs = qb % 3
            P.dma("sp", qTb[qs], qT_s[:, :, qb * 128:(qb + 1) * 128].rearrange("j p t -> p j t"), reads=[nm + "qT_s%d" % (qb // 4)], writes=[nm + "qTb%d" % qs])
            kbs = [kb for kb in (qb - 1, qb, qb + 1) if 0 <= kb < NB]
            a_i = qb % 2
            for hk in range(4):
                g = grp % 2
                grp += 1
                for ki, kb in enumerate(kbs):
                    for u in range(2):
                        bank = (ki * 2 + u) % 2
                        pt = pT[g][ki * 2 + u]
                        ptn = nm + "pT%d_%d" % (g, ki * 2 + u)
                        first = True
                        if kb != qb:
                            mi = 0 if kb < qb else 1
                            P.op("pe", lambda e, bank=bank, mi=mi: e.matmul(ps[bank], lhsT=ident_b, rhs=mb[mi], start=True, stop=False),
                                 reads=["ident_b", nm + "mb%d" % mi], writes=["ps%d" % bank])
                            first = False
                        P.op("pe", lambda e, bank=bank, u=u, kb=kb, hk=hk, qs=qs, first=first: e.matmul(ps[bank], lhsT=kT[u * 64:(u + 1) * 64, hk, kb * 128:(kb + 1) * 128], rhs=qTb[qs][u * 64:(u + 1) * 64, 4 * hk:4 * hk + 4, :], start=first, stop=True),
                             reads=[nm + "kT%d" % (kb // 4), nm + "qTb%d" % qs], writes=["ps%d" % bank])
                        P.op("act", lambda e, pt=pt, bank=bank: e.activation(out=pt, in_=ps[bank], func=AF.Exp, scale=0.125),
                             reads=["ps%d" % bank], writes=[ptn])
                n_mm = len(kbs) * 2
                c_mm = 0
                for ki, kb in enumerate(kbs):
                    sl = kb % 4
                    for u in range(2):
                        pt = pT[g][ki * 2 + u]
                        ptn = nm + "pT%d_%d" % (g, ki * 2 + u)
                        P.op("pe", lambda e, sl=sl, hk=hk, u=u, pt=pt, c_mm=c_mm, n_mm=n_mm: e.matmul(ps[4], lhsT=vp[sl][:, hk, u, :], rhs=pt, start=(c_mm == 0), stop=(c_mm == n_mm - 1)),
                             reads=[nm + "vp%d" % sl, ptn], writes=["ps4"])
                        P.op("pe", lambda e, u=u, pt=pt, c_mm=c_mm, n_mm=n_mm: e.matmul(ps[5], lhsT=onesp[:, u, :], rhs=pt, start=(c_mm == 0), stop=(c_mm == n_mm - 1)),
                             reads=[nm + "ones", ptn], writes=["ps5"])
                        c_mm += 1
                dn = den[g]
                dnn = nm + "den%d" % g
                P.op("dve", lambda e, dn=dn, hk=hk: e.tensor_tensor(out=dn.rearrange("p (j t) -> p j t", j=4), in0=ps[5].rearrange("p (j t) -> p j t", j=4), in1=ske[:, 4 * hk:4 * hk + 4].unsqueeze(2).broadcast_to([128, 4, 128]), op=ALU.add),
                     reads=["ps5", nm + "ske"], writes=[dnn])
                P.op("dve", lambda e, dn=dn: e.reciprocal(out=dn, in_=dn), reads=[dnn], writes=[dnn])
                P.op("dve", lambda e, dn=dn, hk=hk, a_i=a_i: e.tensor_tensor(out=aT[a_i][:, 4 * hk:4 * hk + 4, :], in0=ps[4].rearrange("p (j t) -> p j t", j=4), in1=dn.rearrange("p (j t) -> p j t", j=4), op=ALU.mult),
                     reads=["ps4", dnn], writes=[nm + "aT%d_%d" % (a_i, hk)])
            for cb in range(4):
                bank = 6 + cb % 2
                for jj in range(16):
                    P.op("pe", lambda e, jj=jj, cb=cb, bank=bank, a_i=a_i: e.matmul(ps[bank], lhsT=aT[a_i][:, jj, :], rhs=wo[:, jj, cb * 512:(cb + 1) * 512], start=(jj == 0), stop=(jj == 15)),
                         reads=([nm + "aT%d_%d" % (a_i, hk) for hk in range(4)] + wo_res if jj in (0, 15) else []), writes=["ps%d" % bank])
                s = rs_i % 4
                rs_i += 1
                P.dma("act", rst[s], h[qb * 128:(qb + 1) * 128, cb * 512:(cb + 1) * 512], reads=[hres(qb)[cb]], writes=[nm + "rst%d" % s])
                P.op("dve", lambda e, s=s, bank=bank: e.tensor_tensor(out=rst[s], in0=ps[bank], in1=rst[s], op=ALU.add),
                     reads=["ps%d" % bank, nm + "rst%d" % s], writes=[nm + "rst%d" % s])
                P.dma("act", h[qb * 128:(qb + 1) * 128, cb * 512:(cb + 1) * 512], rst[s], reads=[nm + "rst%d" % s], writes=[hres(qb)[cb]])
        A.pop()
        A.pop()

    def declare_s5(self, dram_in):
        nc, T = self.nc, self.T
        T["s5_norm"] = dram_in("s5_norm", [2, D])
        T["s5_a_re"] = dram_in("s5_a_re", [2, 2, 128, 64])
        T["s5_a_im"] = dram_in("s5_a_im", [2, 2, 128, 64])
        T["s5_log_step"] = dram_in("s5_log_step", [2, 2, 128])
        T["s5_b_re"] = dram_in("s5_b_re", [2, 2, 128, 64, 16])
        T["s5_b_im"] = dram_in("s5_b_im", [2, 2, 128, 64, 16])
        T["s5_c_re"] = dram_in("s5_c_re", [2, 2, 128, 16, 64])
        T["s5_c_im"] = dram_in("s5_c_im", [2, 2, 128, 16, 64])
        T["s5_d"] = dram_in("s5_d", [2, D])
        T["s5_w_glu_a"] = dram_in("s5_w_glu_a", [2, D, D])
        T["s5_w_glu_b"] = dram_in("s5_w_glu_b", [2, D, D])
        T["wa_s"] = nc.dram_tensor("wa_s", [2, 128, 16, 2048], BF16).ap()
        T["wb_s"] = nc.dram_tensor("wb_s", [2, 128, 16, 2048], BF16).ap()
        kd = "ExternalOutput" if self.cfg.get("dbg") else "Internal"
        if self.cfg.get("dbg"):
            T["dbg_v0"] = nc.dram_tensor("dbg_v0", [128, 64 * 2 * 260], BF16, kind="ExternalOutput").ap()
            T["dbg_v1"] = nc.dram_tensor("dbg_v1", [128, 64 * 2 * 260], BF16, kind="ExternalOutput").ap()
            T["dbg_tab"] = nc.dram_tensor("dbg_tab", [10, 128, 1024], F32, kind="ExternalOutput").ap()
            T["dbg_par"] = nc.dram_tensor("dbg_par", [8, 128, 128], F32, kind="ExternalOutput").ap()
        T["u_s"] = nc.dram_tensor("u_s", [2, 128, 128, 256], BF16, kind=kd).ap()
        T["z_s"] = nc.dram_tensor("z_s", [2, 128, 128, 256], BF16, kind=kd).ap()
        T["xf_s"] = nc.dram_tensor("xf_s", [128, 128, 512], BF16).ap()

    def precast_s5(self, j):
        P, T = self.P, self.T
        for (src_n, dst_n) in (("s5_w_glu_a", "wa_s"), ("s5_w_glu_b", "wb_s")):
            src = T[src_n][j].rearrange("(kt p) c -> p kt c", p=128)
            for kt0 in range(0, 16, 4):
                P.dma("pool", T[dst_n][j][:, kt0:kt0 + 4, :], src[:, kt0:kt0 + 4, :], writes=["%s%d_%d" % (dst_n, j, kt0)])

    def s5_phase(self, j, h, A):
        import math
        P, T, ps, psb, pst = self.P, self.T, self.ps, self.psb, self.pst
        ident_b, ident_f = self.ident_b, self.ident_f
        psb2 = [pst[:, (4 + i) * 512:(5 + i) * 512].bitcast(BF16) for i in range(2)]
        nm = "s5%d" % j
        u_s, z_s, xf_s = T["u_s"], T["z_s"], T["xf_s"]
        PI = math.pi
        TT = ALU
        A.push()

        def dve(fn, r, w):
            return P.op("dve", fn, reads=r, writes=w)

        def pool(fn, r, w):
            return P.op("pool", fn, reads=r, writes=w)

        def act(fn, r, w):
            return P.op("act", fn, reads=r, writes=w)

        A.push()
        gain = f32(A, D)
        hst = [f32(A, D) for _ in range(2)]
        xn = [b16(A, D) for _ in range(2)]
        junkb = b16(A, D)
        stat = f32(A, 8)
        P.dma("sp", gain, T["s5_norm"][j:j + 1, :].partition_broadcast(128), writes=[nm + "gain"])
        ust = b16(A, 128 * 256).rearrange("p (g s h) -> p g s h", g=128, s=16)
        c = {"n": 0, "tp": 0}
        hfold = h.rearrange("(j s) c -> j s c", s=16)
        for jt in range(2):
            for sx in range(16):
                s = self.norm_tile(nm, h, 0, gain, hst, xn, junkb, stat, None, c, rows=hfold[jt * 128:(jt + 1) * 128, sx, :], rres=[])
                eng = "act" if sx % 2 == 0 else "pool"
                if eng == "act":
                    act(lambda e, s=s, sx=sx: e.copy(out=ust[:, :, sx, :], in_=xn[s].rearrange("p (g h) -> p g h", h=16)), [nm + "xn%d" % s], [nm + "ust%d" % sx])
                else:
                    pool(lambda e, s=s, sx=sx: e.tensor_copy(out=ust[:, :, sx, :], in_=xn[s].rearrange("p (g h) -> p g h", h=16)), [nm + "xn%d" % s], [nm + "ust%d" % sx])
            P.dma("sp", u_s[jt].rearrange("p g m -> p (g m)"), ust.rearrange("p g s h -> p (g s h)"), reads=[nm + "ust%d" % sx for sx in range(16)], writes=[nm + "u_s%d" % jt])
        A.pop()
        P.barrier()
        if self.cfg.get("s5_stop") == 0:
            raise _Stop()

        def par():
            return f32(A, 128)
        ar, ai, stp, ea, th = par(), par(), par(), par(), par()
        Are, Aim, fr, fi = par(), par(), par(), par()
        w1, w2, w3, w4 = par(), par(), par(), par()
        tw128 = par()
        kB = f32(A, 16)
        kC = f32(A, 16)
        kQ = f32(A, 16)
        maskF = [f32(A, 256) for _ in range(2)]
        maskB = [f32(A, 256) for _ in range(2)]
        dsk = f32(A, 128)
        tin = f32(A, 128)
        tinb = b16(A, 128)
        for (src_n, dst, key) in (("s5_a_re", ar, "ar"), ("s5_a_im", ai, "ai")):
            for d in range(2):
                P.dma("sp", dst[d * 64:(d + 1) * 64, :], T[src_n][j][d].rearrange("g p -> p g"), writes=[nm + key], allow_slow_non_contiguous=True)
        for d in range(2):
            P.dma("sp", w1[d * 64:(d + 1) * 64, :], T["s5_log_step"][j][d:d + 1, :].partition_broadcast(64), writes=[nm + "lst%d" % d])
        act(lambda e: e.activation(out=stp, in_=w1, func=AF.Exp), [nm + "lst0", nm + "lst1"], [nm + "stp"])
        dve(lambda e: e.tensor_tensor(out=ea, in0=ar, in1=stp, op=ALU.mult), [nm + "ar", nm + "stp"], [nm + "ea"])
        dve(lambda e: e.tensor_tensor(out=th, in0=ai, in1=stp, op=ALU.mult), [nm + "ai", nm + "stp"], [nm + "th"])

        wi = A.alloc(4 * 128, I32)
        w5 = par()

        def cis(dst_re, dst_im, k, shiftn, key):
            for (off, dst, kk) in ((0.0, dst_im, "i"), (0.25, dst_re, "r")):
                dve(lambda e, off=off: e.tensor_scalar(out=w2, in0=th, scalar1=float(k) / (2.0 * PI), scalar2=float(shiftn) + off, op0=ALU.mult, op1=ALU.add), [nm + "th"], [nm + "w2"])
                self.turns_to_sin(nm + "cis", w2, dst, wi, w4, w5, [nm + "w2"], [nm + key + kk])
            act(lambda e: e.activation(out=w3, in_=ea, func=AF.Exp, scale=float(k)), [nm + "ea"], [nm + "w3"])
            dve(lambda e: e.tensor_tensor(out=dst_re, in0=dst_re, in1=w3, op=ALU.mult), [nm + key + "r", nm + "w3"], [nm + key + "r"])
            dve(lambda e: e.tensor_tensor(out=dst_im, in0=dst_im, in1=w3, op=ALU.mult), [nm + key + "i", nm + "w3"], [nm + key + "i"])

        cis(Are, Aim, 16, 1, "A")
        lbr, lbi = par(), par()
        cis(lbr, lbi, 1, 1, "lb")
        dve(lambda e: e.tensor_scalar(out=lbr, in0=lbr, scalar1=-1.0, scalar2=None, op0=ALU.add), [nm + "lbr"], [nm + "lbr"])
        dve(lambda e: e.tensor_tensor(out=w1, in0=ar, in1=ar, op=ALU.mult), [nm + "ar", nm + "stp"], [nm + "w1"])
        dve(lambda e: e.tensor_tensor(out=w2, in0=ai, in1=ai, op=ALU.mult), [nm + "ai"], [nm + "w2"])
        dve(lambda e: e.tensor_tensor(out=w1, in0=w1, in1=w2, op=ALU.add), [nm + "w1", nm + "w2"], [nm + "w1"])
        dve(lambda e: e.reciprocal(out=w1, in_=w1), [nm + "w1"], [nm + "w1"])
        dve(lambda e: e.tensor_tensor(out=w2, in0=lbr, in1=ar, op=ALU.mult), [nm + "lbr", nm + "ar"], [nm + "w2"])
        dve(lambda e: e.tensor_tensor(out=w3, in0=lbi, in1=ai, op=ALU.mult), [nm + "lbi", nm + "ai"], [nm + "w3"])
        dve(lambda e: e.tensor_tensor(out=w2, in0=w2, in1=w3, op=ALU.add), [nm + "w2", nm + "w3"], [nm + "w2"])
        dve(lambda e: e.tensor_tensor(out=fr, in0=w2, in1=w1, op=ALU.mult), [nm + "w2", nm + "w1"], [nm + "fr"])
        dve(lambda e: e.tensor_tensor(out=w2, in0=lbi, in1=ar, op=ALU.mult), [nm + "lbi", nm + "ar"], [nm + "w2"])
        dve(lambda e: e.tensor_tensor(out=w3, in0=lbr, in1=ai, op=ALU.mult), [nm + "lbr", nm + "ai"], [nm + "w3"])
        dve(lambda e: e.tensor_tensor(out=w2, in0=w2, in1=w3, op=ALU.subtract), [nm + "w2", nm + "w3"], [nm + "w2"])
        dve(lambda e: e.tensor_tensor(out=fi, in0=w2, in1=w1, op=ALU.mult), [nm + "w2", nm + "w1"], [nm + "fi"])
        for (kt, key, f_base, f_step, b_base, b_step) in ((kB, "kB", 15, -1, 0, 1), (kC, "kC", 1, 1, 16, -1), (kQ, "kQ", -15, 1, 0, -1)):
            pool(lambda e, kt=kt, f_base=f_base, f_step=f_step: e.iota(kt[0:64, :], pattern=[[f_step, 16]], base=f_base, channel_multiplier=0, allow_small_or_imprecise_dtypes=True), [], [nm + key + "f"])
            pool(lambda e, kt=kt, b_base=b_base, b_step=b_step: e.iota(kt[64:128, :], pattern=[[b_step, 16]], base=b_base, channel_multiplier=0, allow_small_or_imprecise_dtypes=True), [], [nm + key + "b"])
        kres = [nm + k + x for k in ("kB", "kC", "kQ") for x in ("f", "b")]
        for a in range(2):
            pool(lambda e, a=a: e.memset(maskF[a], 1.0), [], [nm + "mF%d" % a])
            pool(lambda e, a=a: e.affine_select(out=maskF[a].rearrange("p (t h) -> p t h", h=16), in_=maskF[a].rearrange("p (t h) -> p t h", h=16), pattern=[[16, 16], [0, 16]], compare_op=ALU.is_ge, fill=0.0, base=15 - 128 * a, channel_multiplier=-1),
                 [nm + "mF%d" % a], [nm + "mF%d" % a])
            pool(lambda e, a=a: e.memset(maskB[a], 1.0), [], [nm + "mB%d" % a])
            pool(lambda e, a=a: e.affine_select(out=maskB[a].rearrange("p (t h) -> p t h", h=16), in_=maskB[a].rearrange("p (t h) -> p t h", h=16), pattern=[[-16, 16], [0, 16]], compare_op=ALU.is_ge, fill=0.0, base=128 * a, channel_multiplier=1),
                 [nm + "mB%d" % a], [nm + "mB%d" % a])
        mres = [nm + "m%s%d" % (x, a) for x in "FB" for a in range(2)]
        for t8 in range(8):
            P.dma("sp", dsk[t8 * 16:(t8 + 1) * 16, :], T["s5_d"][j, :].rearrange("(g h) -> h g", h=16), writes=[nm + "dsk%d" % t8], allow_slow_non_contiguous=True)
        dres = [nm + "dsk%d" % t8 for t8 in range(8)]
        P.barrier()
        if self.cfg.get("s5_stop") == 1:
            for pi_, pt_ in enumerate((ar, ai, stp, ea, th, Are, Aim, fr)):
                P.dma("sp", T["dbg_par"][pi_], pt_, reads=[], writes=["dbg_p%d" % pi_])
            raise _Stop()

        def do_half(hf):
            hn = nm + "h%d" % hf
            g0 = hf * 64
            A.push()
            tabs = {}
            for key in ("PBr", "PBi", "PCr", "PCi", "PQr", "PQi"):
                tabs[key] = f32(A, 1024).rearrange("p (g s) -> p g s", s=16)
            bbr = f32(A, 1024).rearrange("p (g s) -> p g s", s=16)
            bbi = f32(A, 1024).rearrange("p (g s) -> p g s", s=16)
            crT = f32(A, 1024).rearrange("p (g s) -> p g s", s=16)
            ciT = f32(A, 1024).rearrange("p (g s) -> p g s", s=16)
            t1 = f32(A, 1024)
            t2 = f32(A, 1024)
            WBr = b16(A, 1024)
            WBi = b16(A, 1024)
            pti = A.alloc(4096, I32).rearrange("p (g s) -> p g s", s=16)
            ptf = f32(A, 1024).rearrange("p (g s) -> p g s", s=16)
            A4 = f32(A, 256).rearrange("p (c g) -> p c g", c=4)
            Vbuf = b16(A, 64 * 2 * 260).rearrange("p (g c w) -> p g c w", g=64, c=2)
            th_h = th[:, g0:g0 + 64]
            ea_h = ea[:, g0:g0 + 64]
            ptm = f32(A, 1024).rearrange("p (g s) -> p g s", s=16)
            big_shift = PI + 2.0 * PI * 64
            t1v = t1[:, 0:1024].rearrange("p (g s) -> p g s", s=16)
            t2v = t2[:, 0:1024].rearrange("p (g s) -> p g s", s=16)
            for (kt, kr, ki) in ((kB, "PBr", "PBi"), (kC, "PCr", "PCi"), (kQ, "PQr", "PQi")):
                kb_ = kt.unsqueeze(1).broadcast_to([128, 64, 16])
                dve(lambda e, kb_=kb_: e.tensor_tensor(out=t1v, in0=th_h.unsqueeze(2).broadcast_to([128, 64, 16]), in1=kb_, op=ALU.mult), [nm + "th"] + kres, [hn + "t1"])
                for (off, dst) in ((0.0, tabs[ki]), (0.25, tabs[kr])):
                    dve(lambda e, off=off: e.tensor_scalar(out=t2[:, 0:1024], in0=t1[:, 0:1024], scalar1=1.0 / (2.0 * PI), scalar2=64.0 + off, op0=ALU.mult, op1=ALU.add), [hn + "t1", hn + "pwtf"], [hn + "t2"])
                    self.turns_to_sin(hn + "pw", t2v, dst, pti, ptf, ptm, [hn + "t2"], [hn + "tab"])
                dve(lambda e, kb_=kb_: e.tensor_tensor(out=t1v, in0=ea_h.unsqueeze(2).broadcast_to([128, 64, 16]), in1=kb_, op=ALU.mult), [nm + "ea"] + kres, [hn + "t1"])
                act(lambda e: e.activation(out=t1v, in_=t1v, func=AF.Exp), [hn + "t1"], [hn + "t1"])
                for key in (kr, ki):
                    dve(lambda e, key=key: e.tensor_tensor(out=tabs[key], in0=tabs[key], in1=t1v, op=ALU.mult), [hn + "tab", hn + "t1"], [hn + "tab"])
            for (src_n, dst, key) in (("s5_b_re", t1v, "t1"), ("s5_b_im", t2v, "t2")):
                for d in range(2):
                    P.dma("sp", dst[d * 64:(d + 1) * 64, :, :], T[src_n][j][d, g0:g0 + 64].rearrange("g p h -> p g h"), reads=[], writes=[hn + key], allow_slow_non_contiguous=True)
            bres = [hn + "t1", hn + "t2"]
            frb = fr[:, g0:g0 + 64].unsqueeze(2).broadcast_to([128, 64, 16])
            fib = fi[:, g0:g0 + 64].unsqueeze(2).broadcast_to([128, 64, 16])
            dve(lambda e: e.tensor_tensor(out=bbr, in0=t1v, in1=frb, op=ALU.mult), bres + [nm + "fr", hn + "t1"], [hn + "bbr"])
            dve(lambda e: e.tensor_tensor(out=ptf, in0=t2v, in1=fib, op=ALU.mult), bres + [nm + "fi", hn + "t2"], [hn + "pwtf"])
            dve(lambda e: e.tensor_tensor(out=bbr, in0=bbr, in1=ptf, op=ALU.subtract), [hn + "bbr", hn + "pwtf"], [hn + "bbr"])
            dve(lambda e: e.tensor_tensor(out=bbi, in0=t2v, in1=frb, op=ALU.mult), bres + [nm + "fr", hn + "t2"], [hn + "bbi"])
            dve(lambda e: e.tensor_tensor(out=ptm, in0=t1v, in1=fib, op=ALU.mult), bres + [nm + "fi", hn + "t1"], [hn + "pwtm"])
            dve(lambda e: e.tensor_tensor(out=bbi, in0=bbi, in1=ptm, op=ALU.add), [hn + "bbi", hn + "pwtm"], [hn + "bbi"])
            for (src_n, dst, key) in (("s5_c_re", crT, "crT"), ("s5_c_im", ciT, "ciT")):
                for g8 in range(8):
                    gs = g0 + g8 * 8
                    P.dma("sp", tin.rearrange("q (d p) -> q d p", d=2), T[src_n][j][:, gs:gs + 8].rearrange("d g h p -> (g h) d p"), reads=[], writes=[nm + "tin"])
                    dve(lambda e: e.tensor_copy(out=tinb, in_=tin), [nm + "tin"], [nm + "tinb"])
                    P.op("pe", lambda e: e.transpose(psb[0][:, 0:128], tinb, ident_b), reads=[nm + "tinb", "ident_b"], writes=["ps2"])
                    act(lambda e, dst=dst, g8=g8: e.copy(out=dst[:, g8 * 8:(g8 + 1) * 8, :], in_=psb[0][:, 0:128].rearrange("p (g h) -> p g h", h=16)), ["ps2"], [hn + key])
            for cc, srcA in enumerate((Are, Are, Aim, Aim)):
                dve(lambda e, cc=cc, srcA=srcA: e.tensor_copy(out=A4[:, cc, :], in_=srcA[:, g0:g0 + 64]), [nm + "Ar", nm + "Ai"], [hn + "A4"])
            pool(lambda e: e.memset(Vbuf[0:64, :, :, 0:3], 0.0), [], [hn + "Vpad0"])
            pool(lambda e: e.memset(Vbuf[64:128, :, :, 257:260], 0.0), [], [hn + "Vpad1"])
            pool(lambda e: e.memset(Vbuf[64:128, :, :, 0:1], 0.0), [], [hn + "Vpad2"])

            def gen_WB(gb):
                gsl = slice(gb * 4, gb * 4 + 4)
                o4 = lambda ap: ap.rearrange("p (g s h) -> p g s h", g=4, s=16)
                pb_r = tabs["PBr"][:, gsl, :].unsqueeze(3).broadcast_to([128, 4, 16, 16])
                pb_i = tabs["PBi"][:, gsl, :].unsqueeze(3).broadcast_to([128, 4, 16, 16])
                b_r = bbr[:, gsl, :].unsqueeze(2).broadcast_to([128, 4, 16, 16])
                b_i = bbi[:, gsl, :].unsqueeze(2).broadcast_to([128, 4, 16, 16])
                dve(lambda e: e.tensor_tensor(out=o4(t1), in0=pb_r, in1=b_r, op=ALU.mult), [hn + "tab", hn + "bbr"], [hn + "t1"])
                dve(lambda e: e.tensor_tensor(out=o4(t2), in0=pb_i, in1=b_i, op=ALU.mult), [hn + "tab", hn + "bbi"], [hn + "t2"])
                dve(lambda e: e.tensor_tensor(out=WBr, in0=t1, in1=t2, op=ALU.subtract), [hn + "t1", hn + "t2"], [hn + "WBr"])
                dve(lambda e: e.tensor_tensor(out=o4(t1), in0=pb_r, in1=b_i, op=ALU.mult), [hn + "tab", hn + "bbi"], [hn + "t1"])
                dve(lambda e: e.tensor_tensor(out=o4(t2), in0=pb_i, in1=b_r, op=ALU.mult), [hn + "tab", hn + "bbr"], [hn + "t2"])
                dve(lambda e: e.tensor_tensor(out=WBi, in0=t1, in1=t2, op=ALU.add), [hn + "t1", hn + "t2"], [hn + "WBi"])

            if self.cfg.get("s5_stop") == 2:
                P.barrier()
                for ti_, key in enumerate(("PBr", "PBi", "PCr", "PCi", "PQr", "PQi")):
                    P.dma("sp", T["dbg_tab"][ti_], tabs[key].rearrange("p g s -> p (g s)"), reads=[], writes=["dbg_t%d" % ti_])
                P.dma("sp", T["dbg_tab"][6], bbr.rearrange("p g s -> p (g s)"), reads=[], writes=["dbg_t6"])
                P.dma("sp", T["dbg_tab"][8], crT.rearrange("p g s -> p (g s)"), reads=[], writes=["dbg_t8"])
                raise _Stop()
            A.push()
            ufold_all = [[b16(A, 16 * 256).rearrange("p (g s h) -> p g s h", g=16, s=16) for _ in range(2)] for _ in range(2)]
            Xf = [b16(A, 512).rearrange("p (a j) -> p a j", a=2) for _ in range(3)]
            WBl = [b16(A, 512).rearrange("p (a c m) -> p a c m", a=2, c=2) for _ in range(2)]
            cnt = {"x": 0, "w": 0, "tp": 0, "v": 0}
            def p1_batch(gb):
                ub = gb // 4
                ufold = ufold_all[ub % 2]
                ufn = hn + "uf%d_" % (ub % 2)
                if gb % 4 == 0:
                    for jt in range(2):
                        P.dma("sp", ufold[jt].rearrange("p g s h -> p (g s h)"), u_s[jt][:, g0 + ub * 16:g0 + (ub + 1) * 16, :].rearrange("p g m -> p (g m)"),
                              reads=[nm + "u_s%d" % jt], writes=[ufn + "%d" % jt])
                gen_WB(gb)
                for gl in range(4):
                    g = gb * 4 + gl
                    gu = g - ub * 16
                    gg = g0 + g
                    xs = cnt["x"] % 3
                    cnt["x"] += 1
                    pb = cnt["tp"] % 2
                    cnt["tp"] += 1
                    for a in range(2):
                        for jt in range(2):
                            P.op("pe", lambda e, a=a, jt=jt, gu=gu, pb=pb, ufold=ufold: e.transpose(psb[pb][:, (a * 2 + jt) * 128:(a * 2 + jt + 1) * 128], ufold[jt][:, gu, 8 * a:8 * a + 8, :].rearrange("p s h -> p (s h)"), ident_b),
                                 reads=[ufn + "%d" % jt, "ident_b"], writes=["ps%d" % (2 + pb)])
                    act(lambda e, xs=xs, pb=pb: e.copy(out=Xf[xs].rearrange("p a j -> p (a j)"), in_=psb[pb][:, 0:512]), ["ps%d" % (2 + pb)], [hn + "Xf%d" % xs])
                    P.dma("act", xf_s[gg], Xf[xs].rearrange("p a j -> p (a j)"), reads=[hn + "Xf%d" % xs], writes=[nm + "xf_s%d" % gg])
                    ws = cnt["w"] % 2
                    cnt["w"] += 1
                    wbank = 4 + ws
                    for a in range(2):
                        for cc, Wsrc in enumerate((WBr, WBi)):
                            P.op("pe", lambda e, a=a, cc=cc, Wsrc=Wsrc, gl=gl, ws=ws: e.transpose(psb2[ws][:, (a * 2 + cc) * 128:(a * 2 + cc + 1) * 128], Wsrc[:, gl * 256 + a * 128:gl * 256 + (a + 1) * 128], ident_b),
                                 reads=[hn + "WBr", hn + "WBi", "ident_b"], writes=["ps%d" % wbank])
                    dve(lambda e, ws=ws: e.tensor_copy(out=WBl[ws].rearrange("p a c m -> p (a c m)"), in_=psb2[ws][:, 0:512]), ["ps%d" % wbank], [hn + "WBl%d" % ws])
                    vbank = cnt["v"] % 2
                    cnt["v"] += 1
                    for cc in range(2):
                        for a in range(2):
                            P.op("pe", lambda e, cc=cc, a=a, ws=ws, xs=xs, vbank=vbank: e.matmul(ps[vbank][:, cc * 256:(cc + 1) * 256], lhsT=WBl[ws][:, a, cc, :], rhs=Xf[xs][:, a, :], start=(a == 0), stop=(a == 1)),
                                 reads=[hn + "WBl%d" % ws, hn + "Xf%d" % xs], writes=["ps%d" % vbank])
                    act(lambda e, g=g, vbank=vbank: e.copy(out=Vbuf[0:64, g, :, 3:259], in_=ps[vbank][0:64, :].rearrange("p (c w) -> p c w", c=2)), ["ps%d" % vbank], [hn + "Vf%d" % g])
                    dve(lambda e, g=g, vbank=vbank: e.tensor_copy(out=Vbuf[64:128, g, :, 1:257], in_=ps[vbank][64:128, :].rearrange("p (c w) -> p c w", c=2)), ["ps%d" % vbank], [hn + "Vb%d" % g])
            for gb in range(16):
                p1_batch(gb)
            A.pop()
            P.barrier()
            if self.cfg.get("dbg") and hf == 0:
                P.dma("sp", T["dbg_v0"], Vbuf.rearrange("p g c w -> p (g c w)"), reads=[], writes=["dbg_v0"])
                for ti_, key in enumerate(("PBr", "PBi", "PCr", "PCi", "PQr", "PQi")):
                    P.dma("sp", T["dbg_tab"][ti_], tabs[key].rearrange("p g s -> p (g s)"), reads=[], writes=["dbg_t%d" % ti_])
                P.dma("sp", T["dbg_tab"][6], bbr.rearrange("p g s -> p (g s)"), reads=[], writes=["dbg_t6"])
                P.dma("sp", T["dbg_tab"][7], bbi.rearrange("p g s -> p (g s)"), reads=[], writes=["dbg_t7"])
                P.dma("sp", T["dbg_tab"][8], crT.rearrange("p g s -> p (g s)"), reads=[], writes=["dbg_t8"])
                P.dma("sp", T["dbg_tab"][9], ciT.rearrange("p g s -> p (g s)"), reads=[], writes=["dbg_t9"])
                for pi_, pt_ in enumerate((ar, ai, stp, ea, th, Are, Aim, fr)):
                    P.dma("sp", T["dbg_par"][pi_], pt_, reads=[], writes=["dbg_p%d" % pi_])
                P.barrier()
            if self.cfg.get("s5_stop") == 3:
                raise _Stop()
            A.push()
            st = f32(A, 128).rearrange("p (c g) -> p c g", c=2)
            prod = f32(A, 256).rearrange("p (c g) -> p c g", c=4)
            tmp = f32(A, 128).rearrange("p (c g) -> p c g", c=2)
            pool(lambda e: e.memset(st.rearrange("p c g -> p (c g)"), 0.0), [], [hn + "st0", hn + "st1"])
            for (d, eng, order) in ((0, "dve", range(256)), (1, "pool", range(255, -1, -1))):
                pr = slice(d * 64, (d + 1) * 64)
                sn = hn + "st%d" % d
                off = 3 if d == 0 else 1
                for jj in order:
                    col = jj + off
                    st4 = st[pr].unsqueeze(1).broadcast_to([64, 2, 2, 64])
                    P.op(eng, lambda e, st4=st4, pr=pr: e.tensor_tensor(out=prod[pr].rearrange("p (a c) g -> p a c g", a=2), in0=A4[pr].rearrange("p (a c) g -> p a c g", a=2), in1=st4, op=ALU.mult),
                         reads=[sn, hn + "A4"], writes=[sn + "p"])
                    P.op(eng, lambda e, pr=pr, col=col: e.tensor_tensor(out=tmp[pr], in0=prod[pr, 0:2, :], in1=Vbuf[pr, :, :, col].rearrange("p g c -> p c g"), op=ALU.add),
                         reads=[sn + "p", hn + "Vall%d" % d], writes=[sn + "t"])
                    P.op(eng, lambda e, pr=pr: e.tensor_tensor(out=st[pr, 0, :], in0=tmp[pr, 0, :], in1=prod[pr, 3, :], op=ALU.subtract),
                         reads=[sn + "t", sn + "p"], writes=[sn])
                    P.op(eng, lambda e, pr=pr: e.tensor_tensor(out=st[pr, 1, :], in0=tmp[pr, 1, :], in1=prod[pr, 2, :], op=ALU.add),
                         reads=[sn + "t", sn + "p"], writes=[sn])
                    P.op(eng, lambda e, pr=pr, col=col: e.tensor_copy(out=Vbuf[pr, :, :, col].rearrange("p g c -> p c g"), in_=st[pr]),
                         reads=[sn], writes=[hn + "Vall%d" % d])
            A.pop()
            P.barrier()
            if self.cfg.get("dbg") and hf == 0:
                P.dma("sp", T["dbg_v1"], Vbuf.rearrange("p g c w -> p (g c w)"), reads=[], writes=["dbg_v1"])
                P.barrier()
            if self.cfg.get("s5_stop") == 4:
                raise _Stop()
            A.push()
            ztok_all = [[b16(A, 16 * 256).rearrange("p (g m) -> p g m", g=16) for _ in range(2)] for _ in range(2)]
            Xf = [b16(A, 512).rearrange("p (a j) -> p a j", a=2) for _ in range(3)]
            WCl = b16(A, 4 * 2 * 256).rearrange("p (g c m) -> p g c m", g=4, c=2)
            WQr = [b16(A, 1024) for _ in range(2)]
            WQi = [b16(A, 1024) for _ in range(2)]
            for d_ in range(2):
                zr = slice(64, 128) if d_ == 0 else slice(0, 64)
                pool(lambda e, d_=d_, zr=zr: e.memset(WQr[d_][zr, :], 0.0), [], [hn + "WQr"])
                pool(lambda e, d_=d_, zr=zr: e.memset(WQi[d_][zr, :], 0.0), [hn + "WQr"], [hn + "WQi"])
            WTl = [b16(A, 512).rearrange("p (a m) -> p a m", a=2) for _ in range(2)]
            tT = [f32(A, 256) for _ in range(2)]
            ys = [f32(A, 256) for _ in range(2)]
            g1 = [f32(A, 256) for _ in range(2)]
            dm = f32(A, 128)
            cnt = {"x": 0, "w": 0, "tp": 0, "y": 0, "e": 0}
            def p2_batch(gb):
                ub = gb // 4
                ztok = ztok_all[ub % 2]
                ztn = hn + "zt%d_" % (ub % 2)
                gsl = slice(gb * 4, gb * 4 + 4)
                o4 = lambda ap: ap.rearrange("p (g t h) -> p g t h", g=4, t=16)
                cr_b = crT[:, gsl, :].unsqueeze(2).broadcast_to([128, 4, 16, 16])
                ci_b = ciT[:, gsl, :].unsqueeze(2).broadcast_to([128, 4, 16, 16])
                pc_r = tabs["PCr"][:, gsl, :].unsqueeze(3).broadcast_to([128, 4, 16, 16])
                pc_i = tabs["PCi"][:, gsl, :].unsqueeze(3).broadcast_to([128, 4, 16, 16])
                dve(lambda e: e.tensor_tensor(out=o4(t1), in0=cr_b, in1=pc_r, op=ALU.mult), [hn + "crT", hn + "tab"], [hn + "t1"])
                dve(lambda e: e.tensor_tensor(out=o4(t2), in0=ci_b, in1=pc_i, op=ALU.mult), [hn + "ciT", hn + "tab"], [hn + "t2"])
                dve(lambda e: e.tensor_tensor(out=WCl[:, :, 0, :], in0=t1.rearrange("p (g m) -> p g m", g=4), in1=t2.rearrange("p (g m) -> p g m", g=4), op=ALU.subtract), [hn + "t1", hn + "t2"], [hn + "WCl0"])
                dve(lambda e: e.tensor_tensor(out=o4(t1), in0=cr_b, in1=pc_i, op=ALU.mult), [hn + "crT", hn + "tab"], [hn + "t1"])
                dve(lambda e: e.tensor_tensor(out=o4(t2), in0=ci_b, in1=pc_r, op=ALU.mult), [hn + "ciT", hn + "tab"], [hn + "t2"])
                dve(lambda e: e.tensor_tensor(out=t1, in0=t1, in1=t2, op=ALU.add), [hn + "t1", hn + "t2"], [hn + "t1"])
                dve(lambda e: e.tensor_scalar(out=WCl[:, :, 1, :], in0=t1.rearrange("p (g m) -> p g m", g=4), scalar1=-1.0, scalar2=None, op0=ALU.mult), [hn + "t1"], [hn + "WCl1"])
                pq_r = tabs["PQr"][:, gsl, :].unsqueeze(3).broadcast_to([128, 4, 16, 16])
                pq_i = tabs["PQi"][:, gsl, :].unsqueeze(3).broadcast_to([128, 4, 16, 16])
                dve(lambda e: e.tensor_tensor(out=o4(t1), in0=cr_b, in1=pq_r, op=ALU.mult), [hn + "crT", hn + "tab"], [hn + "t1"])
                dve(lambda e: e.tensor_tensor(out=o4(t2), in0=ci_b, in1=pq_i, op=ALU.mult), [hn + "ciT", hn + "tab"], [hn + "t2"])
                for d_ in range(2):
                    pr_ = slice(d_ * 64, (d_ + 1) * 64)
                    dve(lambda e, d_=d_, pr_=pr_: e.tensor_tensor(out=WQr[d_][pr_, :], in0=t1[pr_, :], in1=t2[pr_, :], op=ALU.subtract), [hn + "t1", hn + "t2"], [hn + "WQr"])
                dve(lambda e: e.tensor_tensor(out=o4(t1), in0=cr_b, in1=pq_i, op=ALU.mult), [hn + "crT", hn + "tab"], [hn + "t1"])
                dve(lambda e: e.tensor_tensor(out=o4(t2), in0=ci_b, in1=pq_r, op=ALU.mult), [hn + "ciT", hn + "tab"], [hn + "t2"])
                dve(lambda e: e.tensor_tensor(out=t1, in0=t1, in1=t2, op=ALU.add), [hn + "t1", hn + "t2"], [hn + "t1"])
                for d_ in range(2):
                    pr_ = slice(d_ * 64, (d_ + 1) * 64)
                    dve(lambda e, d_=d_, pr_=pr_: e.tensor_scalar(out=WQi[d_][pr_, :], in0=t1[pr_, :], scalar1=-1.0, scalar2=None, op0=ALU.mult), [hn + "t1"], [hn + "WQi"])
                gen_WB(gb)
                if self.cfg.get("s5_stop") == 50:
                    raise _Stop()
                for gl in range(4):
                    g = gb * 4 + gl
                    gu = g - ub * 16
                    gg = g0 + g
                    xs = cnt["x"] % 3
                    cnt["x"] += 1
                    P.dma("sp", Xf[xs].rearrange("p a j -> p (a j)"), xf_s[gg], reads=[nm + "xf_s%d" % gg], writes=[hn + "Xf%d" % xs])
                    for d in range(2):
                        pr = slice(d * 64, (d + 1) * 64)
                        for a in range(2):
                            osl = ps[4 + d][:, a * 256:(a + 1) * 256]
                            P.op("pe", lambda e, d=d, a=a, gl=gl, osl=osl: e.matmul(osl, lhsT=WBr[:, gl * 256 + a * 128:gl * 256 + (a + 1) * 128], rhs=WQr[d][:, gl * 256:(gl + 1) * 256], start=True, stop=False),
                                 reads=[hn + "WBr", hn + "WQr"], writes=["ps%d" % (4 + d)])
                            P.op("pe", lambda e, d=d, a=a, gl=gl, osl=osl: e.matmul(osl, lhsT=WBi[:, gl * 256 + a * 128:gl * 256 + (a + 1) * 128], rhs=WQi[d][:, gl * 256:(gl + 1) * 256], start=False, stop=True),
                                 reads=[hn + "WBi", hn + "WQi"], writes=["ps%d" % (4 + d)])
                    ws = cnt["w"] % 2
                    cnt["w"] += 1
                    for a in range(2):
                        dve(lambda e, a=a: e.tensor_tensor(out=tT[a], in0=ps[4][:, a * 256:(a + 1) * 256], in1=maskF[a], op=ALU.mult), ["ps4"] + mres, [hn + "tT%d" % a])
                        dve(lambda e, gg=gg: e.tensor_scalar(out=dm, in0=ident_f, scalar1=dsk[:, gg:gg + 1], scalar2=None, op0=ALU.mult), ["ident_f"] + dres, [hn + "dm"])
                        dve(lambda e, a=a: e.tensor_tensor(out=tT[a][:, a * 128:(a + 1) * 128], in0=tT[a][:, a * 128:(a + 1) * 128], in1=dm, op=ALU.add), [hn + "tT%d" % a, hn + "dm"], [hn + "tT%d" % a])
                        dve(lambda e, a=a: e.tensor_tensor(out=g1[a], in0=ps[5][:, a * 256:(a + 1) * 256], in1=maskB[a], op=ALU.mult), ["ps5"] + mres, [hn + "g1_%d" % a])
                        pool(lambda e, a=a, ws=ws: e.tensor_tensor(out=WTl[ws][:, a, :], in0=tT[a], in1=g1[a], op=ALU.add), [hn + "tT%d" % a, hn + "g1_%d" % a], [hn + "WTl%d_%d" % (ws, a)])
                    if self.cfg.get("s5_stop") == 51:
                        continue
                    ybank = 6 + cnt["y"] % 2
                    cnt["y"] += 1
                    for jt in range(2):
                        for b in range(2):
                            osl = ps[ybank][:, (jt * 2 + b) * 128:(jt * 2 + b + 1) * 128]
                            for a in range(2):
                                P.op("pe", lambda e, a=a, b=b, jt=jt, ws=ws, xs=xs, osl=osl: e.matmul(osl, lhsT=Xf[xs][:, a, jt * 128:(jt + 1) * 128], rhs=WTl[ws][:, a, b * 128:(b + 1) * 128], start=(a == 0), stop=False),
                                     reads=[hn + "WTl%d_%d" % (ws, a), hn + "Xf%d" % xs], writes=["ps%d" % ybank])
                            for cc in range(2):
                                P.op("pe", lambda e, cc=cc, b=b, jt=jt, gl=gl, g=g, osl=osl: e.matmul(osl, lhsT=Vbuf[:, g, cc, 2 + jt * 128:2 + (jt + 1) * 128], rhs=WCl[:, gl, cc, b * 128:(b + 1) * 128], start=False, stop=(cc == 1)),
                                     reads=[hn + "WCl0", hn + "WCl1", hn + "Vall0", hn + "Vall1", hn + "Vpad0", hn + "Vpad1", hn + "Vpad2"], writes=["ps%d" % ybank])
                    for jt in range(2):
                        es = cnt["e"] % 2
                        cnt["e"] += 1
                        yv, gv = ys[es], g1[es]
                        yn, gn = hn + "ys%d" % es, hn + "g1_%d" % es
                        act(lambda e, yv=yv, jt=jt, ybank=ybank: e.copy(out=yv, in_=ps[ybank][:, jt * 256:(jt + 1) * 256]), ["ps%d" % ybank], [yn])
                        dve(lambda e, yv=yv, gv=gv: e.tensor_tensor(out=gv, in0=yv, in1=yv, op=ALU.mult), [yn], [gn])
                        dve(lambda e, gv=gv: e.tensor_scalar(out=gv, in0=gv, scalar1=0.044715, scalar2=1.0, op0=ALU.mult, op1=ALU.add), [gn], [gn])
                        dve(lambda e, yv=yv, gv=gv: e.tensor_tensor(out=gv, in0=gv, in1=yv, op=ALU.mult), [gn, yn], [gn])
                        act(lambda e, gv=gv: e.activation(out=gv, in_=gv, func=AF.Sigmoid, scale=1.5957691216057308), [gn], [gn])
                        dve(lambda e, yv=yv, gv=gv, jt=jt, gu=gu, ztok=ztok: e.tensor_tensor(out=ztok[jt][:, gu, :], in0=yv, in1=gv, op=ALU.mult), [yn, gn], [ztn + "%d" % jt])
                if gb % 4 == 3:
                    for jt in range(2):
                        P.dma("sp", z_s[jt][:, g0 + ub * 16:g0 + (ub + 1) * 16, :].rearrange("p g m -> p (g m)"), ztok[jt].rearrange("p g m -> p (g m)"),
                              reads=[ztn + "%d" % jt], writes=[nm + "z_s%d_%d_%d" % (jt, hf, ub)])
            for gb in range(16):
                p2_batch(gb)
                if self.cfg.get("s5_stop") in (51, 52, 53):
                    raise _Stop()
            A.pop()
            A.pop()
            P.barrier()

        for hf in range(2):
            do_half(hf)
        A.pop()
        if self.cfg.get("s5_stop") == 5:
            raise _Stop()
        A.push()
        wa = b16(A, 16 * 1024).rearrange("p (k c) -> p k c", k=16)
        wb = b16(A, 16 * 1024).rearrange("p (k c) -> p k c", k=16)
        zfold = b16(A, 128 * 256).rearrange("p (g t h) -> p g t h", g=128, t=16)
        zrow = [b16(A, D) for _ in range(2)]
        zTt = [b16(A, 16 * 128).rearrange("p (k t) -> p k t", k=16) for _ in range(2)]
        sg = [f32(A, 512) for _ in range(2)]
        rst = [f32(A, 512) for _ in range(4)]
        hfold = h.rearrange("(j s) c -> j s c", s=16)
        zres = [nm + "z_s%d_%d_%d" % (jt, hf, ub) for jt in range(2) for hf in range(2) for ub in range(4)]
        tp = 0
        rs_i = 0
        it = 0
        for ch in range(2):
            for kt0 in range(0, 16, 4):
                P.dma("sp", wa[:, kt0:kt0 + 4, :], T["wa_s"][j][:, kt0:kt0 + 4, ch * 1024:(ch + 1) * 1024], reads=["wa_s%d_%d" % (j, kt0)], writes=[nm + "wa%d" % kt0])
                P.dma("sp", wb[:, kt0:kt0 + 4, :], T["wb_s"][j][:, kt0:kt0 + 4, ch * 1024:(ch + 1) * 1024], reads=["wb_s%d_%d" % (j, kt0)], writes=[nm + "wb%d" % kt0])
            w_res = [nm + "w%s%d" % (x, k) for x in "ab" for k in range(0, 16, 4)]
            for jt in range(2):
                P.dma("sp", zfold.rearrange("p g t h -> p (g t h)"), z_s[jt].rearrange("p g m -> p (g m)"), reads=zres, writes=[nm + "zfold"])
                for tx in range(16):
                    s = it % 2
                    it += 1
                    if tx % 2 == 0:
                        pool(lambda e, s=s, tx=tx: e.tensor_copy(out=zrow[s].rearrange("p (g h) -> p g h", h=16), in_=zfold[:, :, tx, :]), [nm + "zfold"], [nm + "zrow%d" % s])
                    else:
                        act(lambda e, s=s, tx=tx: e.copy(out=zrow[s].rearrange("p (g h) -> p g h", h=16), in_=zfold[:, :, tx, :]), [nm + "zfold"], [nm + "zrow%d" % s])
                    for kg in range(4):
                        pb = tp % 2
                        tp += 1
                        for kk in range(4):
                            k = kg * 4 + kk
                            P.op("pe", lambda e, k=k, kk=kk, pb=pb, s=s: e.transpose(psb[pb][:, kk * 128:(kk + 1) * 128], zrow[s][:, k * 128:(k + 1) * 128], ident_b),
                                 reads=[nm + "zrow%d" % s, "ident_b"], writes=["ps%d" % (2 + pb)])
                        dst = zTt[s][:, kg * 4:(kg + 1) * 4, :]
                        srcp = psb[pb][:, 0:512].rearrange("p (k t) -> p k t", k=4)
                        if kg % 2 == 0:
                            act(lambda e, dst=dst, srcp=srcp: e.copy(out=dst, in_=srcp), ["ps%d" % (2 + pb)], [nm + "zT%d_%d" % (s, kg)])
                        else:
                            dve(lambda e, dst=dst, srcp=srcp: e.tensor_copy(out=dst, in_=srcp), ["ps%d" % (2 + pb)], [nm + "zT%d_%d" % (s, kg)])
                    zT_res = [nm + "zT%d_%d" % (s, kg) for kg in range(4)]
                    for cb2 in range(2):
                        cb = ch * 2 + cb2
                        ba = 4 + (cb % 2) * 2
                        bb_ = ba + 1
                        for (bank, wmat) in ((ba, wa), (bb_, wb)):
                            for k in range(16):
                                P.op("pe", lambda e, k=k, cb2=cb2, bank=bank, wmat=wmat, s=s: e.matmul(ps[bank], lhsT=zTt[s][:, k, :], rhs=wmat[:, k, cb2 * 512:(cb2 + 1) * 512], start=(k == 0), stop=(k == 15)),
                                     reads=(zT_res + w_res if k in (0, 15) else []), writes=["ps%d" % bank])
                        sgi = cb % 2
                        act(lambda e, sgi=sgi, bb_=bb_: e.activation(out=sg[sgi], in_=ps[bb_], func=AF.Sigmoid), ["ps%d" % bb_], [nm + "sg%d" % sgi])
                        dve(lambda e, sgi=sgi, ba=ba: e.tensor_tensor(out=sg[sgi], in0=ps[ba], in1=sg[sgi], op=ALU.mult), ["ps%d" % ba, nm + "sg%d" % sgi], [nm + "sg%d" % sgi])
                        r = rs_i % 4
                        rs_i += 1
                        rows = hfold[jt * 128:(jt + 1) * 128, tx, cb * 512:(cb + 1) * 512]
                        hr = nm + "hr%d_%d_%d" % (jt, tx, cb)
                        P.dma("act", rst[r], rows, reads=[hr], writes=[nm + "rst%d" % r])
                        dve(lambda e, r=r, sgi=sgi: e.tensor_tensor(out=rst[r], in0=rst[r], in1=sg[sgi], op=ALU.add), [nm + "rst%d" % r, nm + "sg%d" % sgi], [nm + "rst%d" % r])
                        P.dma("act", rows, rst[r], reads=[nm + "rst%d" % r], writes=[hr])
        A.pop()

    def build(self):
        nc = self.nc
        cfg = self.cfg
        T = {}
        self.T = T
        phases = cfg["phases"]

        def dram_in(name, shape, dt=F32):
            return nc.dram_tensor(name, list(shape), dt, kind="ExternalInput").ap()

        x = dram_in("x", [L, D])
        y = nc.dram_tensor("y", [L, D], F32, kind="ExternalOutput").ap()
        kinds = set(p[0] for p in phases)
        if "mlp" in kinds:
            T["mlp_norm"] = dram_in("mlp_norm", [4, D])
            T["mlp_w_up"] = dram_in("mlp_w_up", [4, D, DFF])
            T["mlp_w_down"] = dram_in("mlp_w_down", [4, DFF, D])
            T["wup_s"] = nc.dram_tensor("wup_s", [4, 32, 128, 16, 256], BF16).ap()
            T["wdn_s"] = nc.dram_tensor("wdn_s", [4, 4, 8, 128, 8, 512], BF16).ap()
        if "attn" in kinds:
            T["positions"] = dram_in("positions", [L], I32)
            T["attn_norm"] = dram_in("attn_norm", [2, D])
            T["attn_w_qkv"] = dram_in("attn_w_qkv", [2, D, 2560])
            T["attn_q_gain"] = dram_in("attn_q_gain", [2, 64])
            T["attn_k_gain"] = dram_in("attn_k_gain", [2, 64])
            T["attn_sink"] = dram_in("attn_sink", [2, 32])
            T["attn_w_o"] = dram_in("attn_w_o", [2, D, D])
            T["wqkv_s"] = nc.dram_tensor("wqkv_s", [2, 128, 16, 2560], BF16).ap()
            T["wo_s"] = nc.dram_tensor("wo_s", [2, 128, 16, 2048], BF16).ap()
            T["qT_s"] = nc.dram_tensor("qT_s", [16, 128, L], BF16).ap()
            T["kT_s"] = nc.dram_tensor("kT_s", [4, 128, L], BF16).ap()
            T["v_s"] = nc.dram_tensor("v_s", [L, 256], BF16).ap()
        if "s5" in kinds:
            self.declare_s5(dram_in)
        import contextlib
        with contextlib.ExitStack() as stk:
            SB = 207 * 1024
            big = stk.enter_context(nc.sbuf_tensor("big", [128, SB], U8))
            pst = stk.enter_context(nc.psum_tensor("pst", [128, 8 * 512], F32))
            A = Arena(big[:], SB)
            self.ps = [pst[:, i * 512:(i + 1) * 512] for i in range(8)]
            self.pst = pst
            self.psb = [pst[:, (2 + i) * 512:(3 + i) * 512].bitcast(BF16) for i in range(2)]
            P = self.P
            ident_f = f32(A, 128)
            ident_b = b16(A, 128)
            self.ident_f, self.ident_b = ident_f, ident_b
            P.op("pool", lambda e: e.memset(ident_f, 0.0), writes=["ident_f"])
            P.op("pool", lambda e: e.affine_select(out=ident_f, in_=ident_f, pattern=[[-1, 128]], compare_op=ALU.not_equal, fill=1.0, base=0, channel_multiplier=1),
                 reads=["ident_f"], writes=["ident_f"])
            P.op("dve", lambda e: e.tensor_copy(out=ident_b, in_=ident_f), reads=["ident_f"], writes=["ident_b"])
            for i0 in range(0, 32, 4):
                P.dma("sp", y[i0 * 128:(i0 + 4) * 128, :], x[i0 * 128:(i0 + 4) * 128, :], writes=[r for i in range(i0, i0 + 4) for r in hres(i)])
            for (k, l) in phases:
                if k == "s5":
                    self.precast_s5(l)
                elif k == "attn":
                    self.precast_attn(l)
            for (k, l) in phases:
                if k == "mlp":
                    self.precast_mlp(l)
            for (k, l) in phases:
                P.barrier()
                if k == "mlp":
                    self.mlp_phase(l, y, y, A)
                elif k == "attn":
                    self.attn_phase(l, y, A)
                elif k == "s5":
                    try:
                        self.s5_phase(l, y, A)
                    except _Stop:
                        pass
            P.emit()
        return nc


PHASES = [("s5", 0), ("mlp", 0), ("attn", 0), ("mlp", 1), ("s5", 1), ("mlp", 2), ("attn", 1), ("mlp", 3)]
_CACHE = {}


def _get_nc():
    if "nc" not in _CACHE:
        nc = bass.Bass("TRN2", target_bir_lowering=False)
        b = Builder(nc, {"phases": PHASES})
        b.build()
        _CACHE["nc"] = nc
    return _CACHE["nc"]


def kernel(**inputs):
    nc = _get_nc()
    x = np.ascontiguousarray(np.asarray(inputs["x"], dtype=np.float32))
    pos = np.ascontiguousarray(np.asarray(inputs["positions"], dtype=np.int32))
    shared = {}
    for k, v in inputs.items():
        if k in ("x", "positions"):
            continue
        shared[k] = np.ascontiguousarray(np.asarray(v, dtype=np.float32))
    in_maps = []
    for c in range(NCORES):
        m = {"x": x[c], "positions": pos[c]}
        m.update(shared)
        in_maps.append(m)
    res = run_bass_kernel_spmd(nc, in_maps, core_ids=list(range(NCORES)))
    out = np.stack([np.asarray(res.results[c]["y"], dtype=np.float32) for c in range(NCORES)], axis=0)
    return out
```
